# Optimizing a Trainium2 kernel written in Bass

```python
import math
import jax, jax.numpy as jnp
from jax import lax
import numpy as np

D_MODEL = 2048
BATCH = 8
SEQ = 2048
DEPTH = 2

A_HEADS = 8
A_HALF_DIM = 64
A_V_DIM = 2 * A_HALF_DIM
B_HEADS = 8
B_NOPE_DIM = 128
B_ROPE_DIM = 64
B_QK_DIM = B_NOPE_DIM + B_ROPE_DIM
B_V_DIM = 128
B_Q_RANK = 512
B_KV_RANK = 256
ROPE_THETA = 10000.0
REL_BUCKETS = 32
REL_MAX_DIST = 128
D_FF = 5632
Q_BLOCK = 128
EPS = 1e-6

N_EVEN = (DEPTH + 1) // 2
N_ODD = DEPTH // 2

A_Q_W = A_HEADS * 2 * A_HALF_DIM
A_K_W = A_HEADS * 2 * A_HALF_DIM
A_V_W = A_HEADS * A_V_DIM
ATTN_IN_W = A_Q_W + A_K_W + A_V_W + B_Q_RANK + B_KV_RANK + B_ROPE_DIM
ATTN_OUT_W = A_HEADS * A_V_DIM + B_HEADS * B_V_DIM
ATTN_SPLITS = [A_Q_W, A_Q_W + A_K_W, A_Q_W + A_K_W + A_V_W,
               A_Q_W + A_K_W + A_V_W + B_Q_RANK,
               A_Q_W + A_K_W + A_V_W + B_Q_RANK + B_KV_RANK]

kernel_name = "hybrid_diffattn_mla_shortconv_encoder"


def rms_norm(x, g):
    xf = x.astype(jnp.float32)
    y = xf * lax.rsqrt(jnp.mean(xf * xf, axis=-1, keepdims=True) + EPS)
    return (y * g.astype(jnp.float32)).astype(x.dtype)


def dwconv3(x, w, b=None):
    xp = jnp.pad(x, ((0, 0), (1, 1), (0, 0)))
    y = w[0] * xp[:, :-2] + w[1] * xp[:, 1:-1] + w[2] * xp[:, 2:]
    return y if b is None else y + b


def t5_bucket(rel):
    nb = REL_BUCKETS // 2
    max_exact = nb // 2
    bucket = jnp.where(rel > 0, nb, 0).astype(jnp.int32)
    n = jnp.abs(rel)
    nf = jnp.maximum(n, max_exact).astype(jnp.float32)
    large = max_exact + (jnp.log(nf / max_exact) / math.log(REL_MAX_DIST / max_exact)
                         * (nb - max_exact)).astype(jnp.int32)
    large = jnp.minimum(large, nb - 1)
    return bucket + jnp.where(n < max_exact, n, large)


def rope_cos_sin(positions):
    inv = 1.0 / (ROPE_THETA ** (jnp.arange(0, B_ROPE_DIM, 2, dtype=jnp.float32) / B_ROPE_DIM))
    ang = positions.astype(jnp.float32)[..., None] * inv
    return jnp.cos(ang), jnp.sin(ang)


def apply_rope(x, cos, sin):
    x1, x2 = jnp.split(x.astype(jnp.float32), 2, axis=-1)
    return jnp.concatenate([x1 * cos - x2 * sin, x2 * cos + x1 * sin], axis=-1).astype(x.dtype)


def diff_attention(q, k, v, positions, rel_table, lam):
    b, h, _, s, dh = q.shape
    nblk = s // Q_BLOCK
    qb = jnp.moveaxis(q.reshape(b, h, 2, nblk, Q_BLOCK, dh), 3, 0)
    starts = jnp.arange(nblk, dtype=jnp.int32) * Q_BLOCK
    scale = dh ** -0.5

    def one_block(args):
        q_blk, start = args
        q_pos = lax.dynamic_slice_in_dim(positions, start, Q_BLOCK, axis=1)
        rel = positions[:, None, :] - q_pos[:, :, None]
        bias = jnp.moveaxis(rel_table[t5_bucket(rel)], -1, 1)
        logits = (jnp.einsum('bhjqd,bhjkd->bhjqk', q_blk, k).astype(jnp.float32) * scale
                  + bias[:, :, None].astype(jnp.float32))
        p = jax.nn.softmax(logits, axis=-1)
        w = p[:, :, 0] - lam * p[:, :, 1]
        return jnp.einsum('bhqk,bhkd->bhqd', w.astype(v.dtype), v)

    out = lax.map(one_block, (qb, starts))
    return jnp.moveaxis(out, 0, 2).reshape(b, h, s, -1)


def mla_attention(q, k, v):
    b, h, s, dqk = q.shape
    nblk = s // Q_BLOCK
    qb = jnp.moveaxis(q.reshape(b, h, nblk, Q_BLOCK, dqk), 2, 0)
    scale = dqk ** -0.5

    def one_block(q_blk):
        logits = jnp.einsum('bhqd,bhkd->bhqk', q_blk, k).astype(jnp.float32) * scale
        p = jax.nn.softmax(logits, axis=-1)
        return jnp.einsum('bhqk,bhkd->bhqd', p.astype(v.dtype), v)

    out = lax.map(one_block, qb)
    return jnp.moveaxis(out, 0, 2).reshape(b, h, s, -1)


def attn_mixer(h, positions, cos, sin, rel_table, layer_idx, w_in, dq_g, dk_g,
               lq1, lk1, lq2, lk2, subln_g, q_a_g, w_uq, kv_a_g, w_ukv, mq_g, mk_g, w_out):
    b, s, _ = h.shape
    aq, ak, av, cq, ckv, kr = jnp.split(h @ w_in, ATTN_SPLITS, axis=-1)

    aq = rms_norm(aq.reshape(b, s, A_HEADS, 2, A_HALF_DIM), dq_g).transpose(0, 2, 3, 1, 4)
    ak = rms_norm(ak.reshape(b, s, A_HEADS, 2, A_HALF_DIM), dk_g).transpose(0, 2, 3, 1, 4)
    av = av.reshape(b, s, A_HEADS, A_V_DIM).transpose(0, 2, 1, 3)
    lam_init = 0.8 - 0.6 * math.exp(-0.3 * layer_idx)
    lam = (jnp.exp(jnp.sum(lq1.astype(jnp.float32) * lk1.astype(jnp.float32)))
           - jnp.exp(jnp.sum(lq2.astype(jnp.float32) * lk2.astype(jnp.float32))) + lam_init)
    oa = diff_attention(aq, ak, av, positions, rel_table, lam)
    oa = rms_norm(oa, subln_g) * (1.0 - lam_init)
    oa = oa.transpose(0, 2, 1, 3).reshape(b, s, A_HEADS * A_V_DIM)

    q = (rms_norm(cq, q_a_g) @ w_uq).reshape(b, s, B_HEADS, B_QK_DIM)
    q = rms_norm(q, mq_g)
    q = jnp.concatenate([q[..., :B_NOPE_DIM],
                         apply_rope(q[..., B_NOPE_DIM:], cos[:, :, None], sin[:, :, None])], axis=-1)
    kv = (rms_norm(ckv, kv_a_g) @ w_ukv).reshape(b, s, B_HEADS, B_NOPE_DIM + B_V_DIM)
    k_nope, vb = jnp.split(kv, [B_NOPE_DIM], axis=-1)
    k_rope = jnp.broadcast_to(kr[:, :, None, :], (b, s, B_HEADS, B_ROPE_DIM))
    k = rms_norm(jnp.concatenate([k_nope, k_rope], axis=-1), mk_g)
    k = jnp.concatenate([k[..., :B_NOPE_DIM],
                         apply_rope(k[..., B_NOPE_DIM:], cos[:, :, None], sin[:, :, None])], axis=-1)
    ob = mla_attention(q.transpose(0, 2, 1, 3), k.transpose(0, 2, 1, 3), vb.transpose(0, 2, 1, 3))
    ob = ob.transpose(0, 2, 1, 3).reshape(b, s, B_HEADS * B_V_DIM)

    return jnp.concatenate([oa, ob], axis=-1) @ w_out


def short_conv_mixer(h, w_in, conv_w, w_out):
    bg, cg, hv = jnp.split(h @ w_in, 3, axis=-1)
    return (bg * dwconv3(cg * hv, conv_w)) @ w_out


def conv_ffn(h, w_gate, w_up, dw_w, dw_b, w_down):
    g = dwconv3(h @ w_gate, dw_w, dw_b)
    return (jax.nn.silu(g) * (h @ w_up)) @ w_down


def setup_inputs(seed: int = 0) -> dict:
    key = jax.random.key(seed)
    ks = iter(jax.random.split(key, 40))

    def dense(shape):
        return jax.random.normal(next(ks), shape, jnp.float32) * (shape[-2] ** -0.5)

    def gain(shape):
        return 1.0 + 0.02 * jax.random.normal(next(ks), shape, jnp.float32)

    def small(shape, scale):
        return scale * jax.random.normal(next(ks), shape, jnp.float32)

    x = jax.random.normal(next(ks), (BATCH, SEQ, D_MODEL), jnp.float32)
    positions = (jnp.arange(SEQ, dtype=jnp.int32)[None, :]
                 + jax.random.randint(next(ks), (BATCH, 1), 0, SEQ, dtype=jnp.int32))
    return {
        "x": x,
        "positions": positions,
        "rel_bias_table": small((REL_BUCKETS, A_HEADS), 0.2),
        "attn_norm_g": gain((N_EVEN, D_MODEL)),
        "attn_w_in": dense((N_EVEN, D_MODEL, ATTN_IN_W)),
        "diff_q_norm_g": gain((N_EVEN, A_HALF_DIM)),
        "diff_k_norm_g": gain((N_EVEN, A_HALF_DIM)),
        "diff_lambda_q1": small((N_EVEN, A_HALF_DIM), 0.1),
        "diff_lambda_k1": small((N_EVEN, A_HALF_DIM), 0.1),
        "diff_lambda_q2": small((N_EVEN, A_HALF_DIM), 0.1),
        "diff_lambda_k2": small((N_EVEN, A_HALF_DIM), 0.1),
        "diff_subln_g": gain((N_EVEN, A_V_DIM)),
        "mla_q_a_norm_g": gain((N_EVEN, B_Q_RANK)),
        "mla_w_uq": dense((N_EVEN, B_Q_RANK, B_HEADS * B_QK_DIM)),
        "mla_kv_a_norm_g": gain((N_EVEN, B_KV_RANK)),
        "mla_w_ukv": dense((N_EVEN, B_KV_RANK, B_HEADS * (B_NOPE_DIM + B_V_DIM))),
        "mla_q_norm_g": gain((N_EVEN, B_QK_DIM)),
        "mla_k_norm_g": gain((N_EVEN, B_QK_DIM)),
        "attn_w_out": dense((N_EVEN, ATTN_OUT_W, D_MODEL)),
        "conv_norm_g": gain((N_ODD, D_MODEL)),
        "conv_w_in": dense((N_ODD, D_MODEL, 3 * D_MODEL)),
        "conv_w": jax.random.normal(next(ks), (N_ODD, 3, D_MODEL), jnp.float32) * (3 ** -0.5),
        "conv_w_out": dense((N_ODD, D_MODEL, D_MODEL)),
        "ffn_norm_g": gain((DEPTH, D_MODEL)),
        "ffn_w_gate": dense((DEPTH, D_MODEL, D_FF)),
        "ffn_w_up": dense((DEPTH, D_MODEL, D_FF)),
        "ffn_dwconv_w": jax.random.normal(next(ks), (DEPTH, 3, D_FF), jnp.float32) * (3 ** -0.5),
        "ffn_dwconv_b": small((DEPTH, D_FF), 0.02),
        "ffn_w_down": dense((DEPTH, D_FF, D_MODEL)),
    }


def reference(x, positions, rel_bias_table, attn_norm_g, attn_w_in, diff_q_norm_g, diff_k_norm_g,
              diff_lambda_q1, diff_lambda_k1, diff_lambda_q2, diff_lambda_k2, diff_subln_g,
              mla_q_a_norm_g, mla_w_uq, mla_kv_a_norm_g, mla_w_ukv, mla_q_norm_g, mla_k_norm_g,
              attn_w_out, conv_norm_g, conv_w_in, conv_w, conv_w_out, ffn_norm_g, ffn_w_gate,
              ffn_w_up, ffn_dwconv_w, ffn_dwconv_b, ffn_w_down):
    cos, sin = rope_cos_sin(positions)
    for layer in range(DEPTH):
        i = layer // 2
        if layer % 2 == 0:
            x = x + attn_mixer(rms_norm(x, attn_norm_g[i]), positions, cos, sin, rel_bias_table, layer,
                               attn_w_in[i], diff_q_norm_g[i], diff_k_norm_g[i],
                               diff_lambda_q1[i], diff_lambda_k1[i], diff_lambda_q2[i], diff_lambda_k2[i],
                               diff_subln_g[i], mla_q_a_norm_g[i], mla_w_uq[i], mla_kv_a_norm_g[i],
                               mla_w_ukv[i], mla_q_norm_g[i], mla_k_norm_g[i], attn_w_out[i])
        else:
            x = x + short_conv_mixer(rms_norm(x, conv_norm_g[i]), conv_w_in[i], conv_w[i], conv_w_out[i])
        x = x + conv_ffn(rms_norm(x, ffn_norm_g[layer]), ffn_w_gate[layer], ffn_w_up[layer],
                         ffn_dwconv_w[layer], ffn_dwconv_b[layer], ffn_w_down[layer])
    return x
```

```python
from contextlib import ExitStack
import math
import os
import numpy as np
import concourse.bass as bass
import concourse.mybir as mybir
from concourse.bass_utils import run_bass_kernel_spmd

F32 = mybir.dt.float32
BF16 = mybir.dt.bfloat16
I32 = mybir.dt.int32
ALU = mybir.AluOpType
AF = mybir.ActivationFunctionType

D = 2048
S = 2048
DFF = 5632
NCH = 44
GRP = 11
EPS = 1e-6
NR = 1536
LAM_INIT0 = 0.8 - 0.6 * math.exp(-0.3 * 0)


class Buf:
    __slots__ = ("w", "r", "sem", "scnt", "excl")

    def __init__(self, excl=False):
        self.excl = excl
        self.w = None
        self.r = {}
        self.sem = None
        self.scnt = 0


class Prog:
    def __init__(self):
        self.nc = bass.Bass("TRN2", target_bir_lowering=False)
        self.es = ExitStack()
        nc = self.nc
        self.eng = dict(pe=nc.tensor, act=nc.scalar, dve=nc.vector, pool=nc.gpsimd, sp=nc.sync)
        self.esem = {k: self.es.enter_context(nc.semaphore("sem_" + k)) for k in self.eng}
        self.ecnt = {k: 0 for k in self.eng}
        self.seen = {k: {} for k in self.eng}
        self.dsems = []
        self.free_sems = []
        self.skeys = [[]]
        self.nds = 0
        self.scope = None

    def sb(self, name, shape, dt):
        st = self.scope if self.scope is not None else self.es
        self.nsb = getattr(self, "nsb", 0) + 1
        return st.enter_context(self.nc.sbuf_tensor("%s_%d" % (name, self.nsb), list(shape), dt))

    def push(self):
        self.scopes = getattr(self, "scopes", [])
        self.scopes.append(self.scope)
        self.scope = ExitStack()
        self.skeys.append([])

    def pop(self):
        self.barrier()
        self.scope.close()
        self.scope = self.scopes.pop()
        for b in self.skeys.pop():
            self.free_sems.append((b.sem, b.scnt))
            self.dsems.remove(b)
            b.sem = None

    def ps(self, name, shape, dt):
        return self.es.enter_context(self.nc.psum_tensor(name, list(shape), dt))

    def _wait(self, e, toks):
        need = {}
        seen = self.seen[e]
        for t in toks:
            if t is None:
                continue
            s, v = t
            k = id(s)
            if seen.get(k, 0) >= v:
                continue
            if k not in need or need[k][1] < v:
                need[k] = (s, v)
        for k, (s, v) in need.items():
            self.eng[e].wait_ge(s, v)
            seen[k] = v

    def _deps(self, reads, writes):
        toks = []
        for b in reads:
            toks.append(b.w)
            if b.excl:
                toks.extend(b.r.values())
        for b in writes:
            toks.append(b.w)
            toks.extend(b.r.values())
        return toks

    def _commit(self, tok, reads, writes):
        s, v = tok
        k = id(s)
        for b in reads:
            o = b.r.get(k)
            if o is None or o[1] < v:
                b.r[k] = tok
        for b in writes:
            b.w = tok
            b.r = {}

    def op(self, e, fn, reads=(), writes=(), inc=True):
        toks = self._deps(reads, writes)
        if e == "pe":
            pes = id(self.esem["pe"])
            toks = [t for t in toks if t is not None and id(t[0]) != pes]
        self._wait(e, toks)
        ins = fn(self.eng[e])
        if inc:
            self.ecnt[e] += 1
            ins.then_inc(self.esem[e], 1)
            tok = (self.esem[e], self.ecnt[e])
        else:
            tok = (self.esem[e], self.ecnt[e] + 1)
        self._commit(tok, reads, writes)
        return tok

    def dma(self, q, out, in_, reads=(), writes=(), key=None, **kw):
        toks = self._deps(reads, writes)
        self._wait(q, toks)
        if key.sem is None:
            if self.free_sems:
                key.sem, key.scnt = self.free_sems.pop()
            else:
                key.sem = self.es.enter_context(self.nc.semaphore("dsem%d" % self.nds))
                self.nds += 1
            self.dsems.append(key)
            self.skeys[-1].append(key)
        key.scnt += 16
        self.eng[q].dma_start(out=out, in_=in_, **kw).then_inc(key.sem, 16)
        tok = (key.sem, key.scnt)
        self._commit(tok, reads, writes)
        return tok

    def barrier(self):
        toks = [(self.esem[k], self.ecnt[k]) for k in self.eng if self.ecnt[k] > 0]
        toks += [(b.sem, b.scnt) for b in self.dsems]
        for e in self.eng:
            mine = id(self.esem[e])
            self._wait(e, [t for t in toks if id(t[0]) != mine])

    def finish(self, bufs, e="sp"):
        self._wait(e, [b.w for b in bufs])


def _st(W, cols, M):
    K = W.shape[0]
    KC = K // 128
    cols = np.asarray(cols)
    n = len(cols) // M
    Wc = W[:, cols].reshape(KC, 128, n, M)
    return np.ascontiguousarray(Wc.transpose(2, 1, 0, 3)).reshape(n, 128, KC * M)


def _pk(v):
    return np.ascontiguousarray(np.asarray(v).reshape(-1, 128).T)


def t5_bucket_np(rel):
    nb = 16
    me = 8
    bucket = np.where(rel > 0, nb, 0).astype(np.int32)
    n = np.abs(rel)
    nf = np.maximum(n, me).astype(np.float32)
    large = me + (np.log(nf / me) / math.log(128 / me) * (nb - me)).astype(np.int32)
    large = np.minimum(large, nb - 1)
    return bucket + np.where(n < me, n, large)


VC = {}
_c = 0
for _n, _w in [("g_attn", 16), ("g_ffn0", 16), ("g_conv", 16), ("g_ffn1", 16), ("dq", 1), ("dk", 1),
               ("subln", 1), ("qa", 4), ("kva", 2), ("mq_n", 1), ("mq_r", 1), ("mq_s", 1),
               ("mk_n", 1), ("mk_r", 1), ("mk_s", 1), ("dk_lo", 1), ("dk_hi", 1), ("cw", 48), ("fw0", 132), ("fw1", 132),
               ("fb0", 44), ("fb1", 44), ("invf", 1), ("sgn", 1), ("quarter", 1)]:
    VC[_n] = _c
    _c += _w
NV = _c


def prep_shared(I):
    f = lambda k: np.asarray(I[k], dtype=np.float32)
    sh = {}
    vec = np.zeros((128, NV), np.float32)
    vec[:, VC["g_attn"]:VC["g_attn"] + 16] = _pk(f("attn_norm_g")[0])
    vec[:, VC["g_ffn0"]:VC["g_ffn0"] + 16] = _pk(f("ffn_norm_g")[0])
    vec[:, VC["g_conv"]:VC["g_conv"] + 16] = _pk(f("conv_norm_g")[0])
    vec[:, VC["g_ffn1"]:VC["g_ffn1"] + 16] = _pk(f("ffn_norm_g")[1])
    p = np.arange(128)
    vec[:, VC["dq"]] = f("diff_q_norm_g")[0][p % 64]
    vec[:, VC["dk"]] = f("diff_k_norm_g")[0][p % 64]
    vec[0:64, VC["dk_lo"]] = f("diff_k_norm_g")[0]
    vec[64:128, VC["dk_hi"]] = f("diff_k_norm_g")[0]
    vec[:, VC["subln"]] = f("diff_subln_g")[0]
    vec[:, VC["qa"]:VC["qa"] + 4] = _pk(f("mla_q_a_norm_g")[0])
    vec[:, VC["kva"]:VC["kva"] + 2] = _pk(f("mla_kv_a_norm_g")[0])
    for nm, key in (("mq", "mla_q_norm_g"), ("mk", "mla_k_norm_g")):
        g = f(key)[0]
        vec[:, VC[nm + "_n"]] = g[:128]
        vec[:, VC[nm + "_r"]] = g[128 + (p % 64)]
        vec[:, VC[nm + "_s"]] = g[128 + ((p % 64) + 32) % 64]
        if nm == "mk":
            vec[64:128, VC["mk_r"]] = 0.0
            vec[64:128, VC["mk_s"]] = 0.0
    vec[:, VC["cw"]:VC["cw"] + 48] = f("conv_w")[0].T.reshape(16, 128, 3).transpose(1, 0, 2).reshape(128, 48)
    for l in range(2):
        vec[:, VC["fw%d" % l]:VC["fw%d" % l] + 132] = (
            f("ffn_dwconv_w")[l].T.reshape(NCH, 128, 3).transpose(1, 0, 2).reshape(128, 132))
        vec[:, VC["fb%d" % l]:VC["fb%d" % l] + 44] = _pk(f("ffn_dwconv_b")[l])
    invf = 1.0 / (10000.0 ** (np.arange(0, 64, 2, dtype=np.float32) / 64))
    vec[:, VC["invf"]] = (invf / np.float32(2 * math.pi))[p % 32]
    vec[:, VC["sgn"]] = np.where((p % 64) < 32, -1.0, 1.0)
    vec[:, VC["quarter"]] = 0.25
    sh["vec"] = vec
    sh["lamv"] = np.concatenate([f("diff_lambda_q1")[0], f("diff_lambda_q2")[0],
                                 f("diff_lambda_k1")[0], f("diff_lambda_k2")[0]])[None, :].copy()
    sh["relt"] = f("rel_bias_table").copy()
    import ml_dtypes
    bf = ml_dtypes.bfloat16
    cb = np.zeros((128, 4 * 128), np.float32)
    cb[0:64, 384:512] = 1.0
    cb[:, 0:128] = 1.0
    cb[0:64, 128:192] = 1.0
    cb[64:128, 192:256] = 1.0
    cb[p, 256 + 127 - p] = 1.0
    sh["cbf"] = cb.astype(bf)
    m = np.arange(NR)
    bk = t5_bucket_np(767 - m)
    oh = np.zeros((32, NR + 256), np.float32)
    oh[bk, m] = 1.0
    oh[15, NR:NR + 128] = 1.0
    oh[31, NR + 128:NR + 256] = 1.0
    sh["oneh"] = oh
    w_in = f("attn_w_in")[0]
    ar = np.arange
    sh["w_in_st"] = _st(w_in, np.concatenate([ar(0, 2048), ar(3072, 3840)]), 128)
    _kr = ar(3840, 3904)
    _ks = np.concatenate([ar(3872, 3904), ar(3840, 3872)])
    sh["w_in_kr"] = _st(w_in, np.concatenate([_kr, _kr, _ks, _ks]), 128)
    sh["w_in_v"] = _st(w_in, ar(2048, 3072), 128)
    uq = f("mla_w_uq")[0]
    sh["w_uq_n"] = _st(uq, np.concatenate([ar(192 * h, 192 * h + 128) for h in range(8)]), 128)
    sh["w_uq_r"] = _st(uq, np.concatenate([np.concatenate([ar(192 * h + 128, 192 * h + 192),
                                                            ar(192 * h + 128, 192 * h + 192),
                                                            ar(192 * h + 160, 192 * h + 192),
                                                            ar(192 * h + 128, 192 * h + 160),
                                                            ar(192 * h + 160, 192 * h + 192),
                                                            ar(192 * h + 128, 192 * h + 160)])
                                           for h in range(8)]), 128)
    ukv = f("mla_w_ukv")[0]
    sh["w_ukv_k"] = _st(ukv, np.concatenate([ar(256 * h, 256 * h + 128) for h in range(8)]), 128)
    sh["w_ukv_v"] = _st(ukv, np.concatenate([ar(256 * h + 128, 256 * h + 256) for h in range(8)]), 128)
    sh["w_out"] = _st(f("attn_w_out")[0], ar(2048), 128)
    cwi = f("conv_w_in")[0]
    sh["cw_in"] = _st(cwi, np.concatenate([np.concatenate([ar(128 * j, 128 * j + 128) + 2048 * t
                                                           for t in range(3)]) for j in range(16)]), 128)
    sh["cw_out"] = _st(f("conv_w_out")[0], ar(2048), 128)
    for l in range(2):
        sh["wg%d" % l] = _st(f("ffn_w_gate")[l], ar(DFF), 128)
        sh["wu%d" % l] = _st(f("ffn_w_up")[l], ar(DFF), 128)
        wd = f("ffn_w_down")[l]
        t = wd.reshape(4, GRP, 128, 16, 128)
        sh["wd%d" % l] = np.ascontiguousarray(t.transpose(0, 3, 2, 1, 4)).reshape(64, 128, GRP * 128)
    return sh


SHAPES = dict(vec=[128, NV], lamv=[1, 256], relt=[32, 8], oneh=[32, NR + 256],
              w_in_st=[22, 128, 2048], w_in_kr=[2, 128, 2048], w_in_v=[8, 128, 2048],
              w_uq_n=[8, 128, 512], w_uq_r=[16, 128, 512], w_ukv_k=[8, 128, 256], w_ukv_v=[8, 128, 256],
              w_out=[16, 128, 2048], cw_in=[48, 128, 2048], cw_out=[16, 128, 2048],
              wg0=[NCH, 128, 2048], wu0=[NCH, 128, 2048], wd0=[64, 128, GRP * 128],
              wg1=[NCH, 128, 2048], wu1=[NCH, 128, 2048], wd1=[64, 128, GRP * 128])


class K:
    def __init__(self, phases):
        self.p = Prog()
        p = self.p
        nc = p.nc
        self.nc = nc
        self.phases = phases
        self.A = {}
        for k, shp in SHAPES.items():
            self.A[k] = nc.dram_tensor(k, shp, F32, kind="ExternalInput").ap()
        self.cbf_d = nc.dram_tensor("cbf", [128, 512], BF16, kind="ExternalInput").ap()
        self.pos_d = nc.dram_tensor("pos", [1, S], I32, kind="ExternalInput").ap()
        self.x_in = nc.dram_tensor("xT", [16, 128, S], F32, kind="ExternalInput").ap()
        self.x_out = nc.dram_tensor("yT", [16, 128, S], F32, kind="ExternalOutput").ap()
        self.xs = [nc.dram_tensor("xs%d" % i, [16, 128, S], F32, kind="Internal").ap() for i in range(3)]
        self.PS = p.ps("PS", [128, 8, 512], F32)
        self.bank = [Buf(excl=True) for _ in range(8)]
        self.vec = p.sb("vec", [128, NV], F32)
        self.cbf = p.sb("cbfs", [128, 512], BF16)
        self.cb = Buf()
        p.dma("sp", self.vec[:], self.A["vec"], writes=[self.cb], key=self.cb)
        self.cb2 = Buf()
        p.dma("sp", self.cbf[:], self.cbf_d, writes=[self.cb2], key=self.cb2)
        self.ones = self.cbf[:, 0:128]
        self.blk = self.cbf[:, 128:256]
        self.anti = self.cbf[:, 256:384]
        self.oneslo = self.cbf[:, 384:512]
        self.wq = 0

    def bk(self, i):
        return self.PS[:, i, :]

    def vcol(self, name, i=0, rows=128):
        c = VC[name] + i
        return self.vec[0:rows, c:c + 1]

    def wload(self, slot, sbuf, src, width):
        p = self.p
        if width > 2048:
            o = sbuf[:, 0:width].rearrange("p (a b) -> p a b", b=2048)
            i = src.rearrange("p (a b) -> p a b", b=2048)
        else:
            o = sbuf[:, 0:width]
            i = src
        p.dma("pool", o, i, writes=[slot], key=slot)

    def norm(self, xsrc, xbufs, gname, hT, hb):
        p = self.p
        p.push()
        xt = [p.sb("nx%d" % i, [128, S], F32) for i in range(2)]
        xtb = [Buf() for _ in range(2)]
        sq = [p.sb("nsq%d" % i, [128, S], BF16) for i in range(2)]
        sqb = [Buf() for _ in range(2)]
        rstd = p.sb("nrstd", [128, S], F32)
        rb = Buf()
        for kc in range(16):
            s = kc % 2
            p.dma("sp", xt[s][:], xsrc[kc], reads=xbufs[kc], writes=[xtb[s]], key=xtb[s])
            p.op("act", lambda e: e.activation(out=sq[s][:], in_=xt[s][:], func=AF.Square),
                 reads=[xtb[s]], writes=[sqb[s]])
            for j in range(4):
                p.op("pe", lambda e: e.matmul(self.bk(j), self.ones, sq[s][:, 512 * j:512 * j + 512],
                                              start=(kc == 0), stop=(kc == 15)),
                     reads=[sqb[s], self.cb2], writes=[self.bank[j]], inc=(j == 3))
        for j in range(4):
            p.op("act", lambda e: e.activation(out=rstd[:, 512 * j:512 * j + 512], in_=self.bk(j),
                                               func=AF.Sqrt, scale=1.0 / D, bias=self.eps_ap()),
                 reads=[self.bank[j]], writes=[rb])
        p.op("dve", lambda e: e.reciprocal(out=rstd[:], in_=rstd[:]), reads=[rb], writes=[rb])
        for kc in range(16):
            s = kc % 2
            p.dma("sp", xt[s][:], xsrc[kc], reads=xbufs[kc], writes=[xtb[s]], key=xtb[s])
            p.op("dve", lambda e: e.scalar_tensor_tensor(out=hT[:, kc, :], in0=xt[s][:],
                                                         scalar=self.vcol(gname, kc), in1=rstd[:],
                                                         op0=ALU.mult, op1=ALU.mult),
                 reads=[xtb[s], rb, self.cb], writes=[hb[kc]])
        p.pop()

    def eps_ap(self):
        if not hasattr(self, "_eps"):
            p = self.p
            st = p.scope
            p.scope = None
            self._eps = p.sb("epsc", [128, 1], F32)
            p.scope = st
            self._epsb = Buf()
            p.op("dve", lambda e: e.memset(self._eps[:], EPS), writes=[self._epsb])
            p.barrier()
        return self._eps[:, 0:1]

    def ffn(self, l, xsrc, xsb, xdst, xdb):
        p = self.p
        p.push()
        hT = p.sb("hT", [128, 16, S], BF16)
        hb = [Buf() for _ in range(16)]
        self.norm(xsrc, xsb, "g_ffn%d" % l, hT, hb)
        act = p.sb("fact", [128, GRP, S], BF16)
        ab = [Buf() for _ in range(GRP)]
        NSL = 3
        wg = [p.sb("fwg%d" % i, [128, 2048], BF16) for i in range(NSL)]
        wu = [p.sb("fwu%d" % i, [128, 2048], BF16) for i in range(NSL)]
        wgb = [Buf() for _ in range(NSL)]
        wub = [Buf() for _ in range(NSL)]
        wd = [p.sb("fwd%d" % i, [128, GRP * 128], BF16) for i in range(NSL)]
        wdb = [Buf() for _ in range(NSL)]
        y = [p.sb("fy%d" % i, [128, S], F32) for i in range(2)]
        yb = [Buf() for _ in range(2)]
        sg = [p.sb("fs%d" % i, [128, S], BF16) for i in range(2)]
        sgb = [Buf() for _ in range(2)]
        xo = [p.sb("fxo%d" % i, [128, 512], F32) for i in range(4)]
        xob = [Buf() for _ in range(4)]
        xn = [p.sb("fxn%d" % i, [128, 512], F32) for i in range(4)]
        xnb = [Buf() for _ in range(4)]
        Wg, Wu, Wd = self.A["wg%d" % l], self.A["wu%d" % l], self.A["wd%d" % l]
        fw, fb = "fw%d" % l, "fb%d" % l
        ev = 0
        self.wload(wgb[0], wg[0], Wg[0], 2048)
        self.wload(wub[0], wu[0], Wu[0], 2048)
        for grp in range(4):
            for cl in range(GRP):
                c = grp * GRP + cl
                s = c % NSL
                if c + 1 < NCH:
                    s1 = (c + 1) % NSL
                    self.wload(wgb[s1], wg[s1], Wg[c + 1], 2048)
                    self.wload(wub[s1], wu[s1], Wu[c + 1], 2048)
                for half, (wt, wtb) in enumerate(((wg[s], wgb[s]), (wu[s], wub[s]))):
                    for j in range(4):
                        bi = half * 4 + j
                        for kc in range(16):
                            p.op("pe", lambda e: e.matmul(self.bk(bi), wt[:, kc * 128:(kc + 1) * 128],
                                                          hT[:, kc, 512 * j:512 * j + 512],
                                                          start=(kc == 0), stop=(kc == 15)),
                                 reads=[wtb, hb[kc]], writes=[self.bank[bi]], inc=(kc == 15))
                ys = c % 2
                yt, ytb = y[ys], yb[ys]
                w0, w1, w2 = (self.vcol(fw, 3 * c + t) for t in range(3))
                bcol = self.vcol(fb, c)
                for j in range(4):
                    p.op("dve", lambda e: e.tensor_scalar(out=yt[:, 512 * j:512 * j + 512], in0=self.bk(j),
                                                          scalar1=w1, scalar2=bcol, op0=ALU.mult, op1=ALU.add),
                         reads=[self.bank[j], self.cb], writes=[ytb])
                for j in range(4):
                    n = 512 if j < 3 else 511
                    p.op("dve", lambda e: e.scalar_tensor_tensor(
                        out=yt[:, 512 * j + 1:512 * j + 1 + n], in0=self.PS[:, j, 0:n], scalar=w0,
                        in1=yt[:, 512 * j + 1:512 * j + 1 + n], op0=ALU.mult, op1=ALU.add),
                         reads=[self.bank[j], ytb], writes=[ytb])
                for j in range(4):
                    lo = 0 if j > 0 else 1
                    n = 512 - lo
                    p.op("dve", lambda e: e.scalar_tensor_tensor(
                        out=yt[:, 512 * j + lo - 1:512 * j + lo - 1 + n], in0=self.PS[:, j, lo:512], scalar=w2,
                        in1=yt[:, 512 * j + lo - 1:512 * j + lo - 1 + n], op0=ALU.mult, op1=ALU.add),
                         reads=[self.bank[j], ytb], writes=[ytb])
                st, stb = sg[ys], sgb[ys]
                p.op("act", lambda e: e.activation(out=st[:], in_=yt[:], func=AF.Silu),
                     reads=[ytb], writes=[stb])
                for j in range(4):
                    p.op("dve", lambda e: e.tensor_tensor(out=act[:, cl, 512 * j:512 * j + 512],
                                                          in0=self.bk(4 + j), in1=st[:, 512 * j:512 * j + 512],
                                                          op=ALU.mult),
                         reads=[self.bank[4 + j], stb], writes=[ab[cl]])
            self.wload(wdb[0], wd[0], Wd[grp * 16 + 0], GRP * 128)
            for dc in range(16):
                s = dc % NSL
                if dc + 1 < 16:
                    s1 = (dc + 1) % NSL
                    self.wload(wdb[s1], wd[s1], Wd[grp * 16 + dc + 1], GRP * 128)
                for j in range(4):
                    bi = ev % 8
                    es = ev % 4
                    ev += 1
                    src = xsrc if grp == 0 else xdst
                    srcb = xsb if grp == 0 else xdb
                    p.dma("sp", xo[es][:], src[dc][:, 512 * j:512 * j + 512], reads=[srcb[dc][j]],
                          writes=[xob[es]], key=xob[es])
                    for cl in range(GRP):
                        p.op("pe", lambda e: e.matmul(self.bk(bi), wd[s][:, cl * 128:(cl + 1) * 128],
                                                      act[:, cl, 512 * j:512 * j + 512],
                                                      start=(cl == 0), stop=(cl == GRP - 1)),
                             reads=[wdb[s], ab[cl]], writes=[self.bank[bi]], inc=(cl == GRP - 1))
                    p.op("dve", lambda e: e.tensor_tensor(out=xn[es][:], in0=self.bk(bi), in1=xo[es][:],
                                                          op=ALU.add),
                         reads=[self.bank[bi], xob[es]], writes=[xnb[es]])
                    p.dma("sp", xdst[dc][:, 512 * j:512 * j + 512], xn[es][:], reads=[xnb[es]],
                          writes=[xdb[dc][j]], key=xnb[es])
        p.pop()

    def outproj(self, Wd, actT, actb, xsrc, xsb, xdst, xdb):
        p = self.p
        NSL = 3
        w = [p.sb("ow%d" % i, [128, 2048], BF16) for i in range(NSL)]
        wb = [Buf() for _ in range(NSL)]
        xo = [p.sb("oxo%d" % i, [128, 512], F32) for i in range(4)]
        xob = [Buf() for _ in range(4)]
        xn = [p.sb("oxn%d" % i, [128, 512], F32) for i in range(4)]
        xnb = [Buf() for _ in range(4)]
        ev = 0
        self.wload(wb[0], w[0], Wd[0], 2048)
        for dc in range(16):
            s = dc % NSL
            if dc + 1 < 16:
                self.wload(wb[(dc + 1) % NSL], w[(dc + 1) % NSL], Wd[dc + 1], 2048)
            for j in range(4):
                bi = ev % 8
                es = ev % 4
                ev += 1
                p.dma("sp", xo[es][:], xsrc[dc][:, 512 * j:512 * j + 512], reads=[xsb[dc][j]],
                      writes=[xob[es]], key=xob[es])
                for kc in range(16):
                    p.op("pe", lambda e: e.matmul(self.bk(bi), w[s][:, kc * 128:(kc + 1) * 128],
                                                  actT[:, kc, 512 * j:512 * j + 512],
                                                  start=(kc == 0), stop=(kc == 15)),
                         reads=[wb[s], actb[kc]], writes=[self.bank[bi]], inc=(kc == 15))
                p.op("dve", lambda e: e.tensor_tensor(out=xn[es][:], in0=self.bk(bi), in1=xo[es][:], op=ALU.add),
                     reads=[self.bank[bi], xob[es]], writes=[xnb[es]])
                p.dma("sp", xdst[dc][:, 512 * j:512 * j + 512], xn[es][:], reads=[xnb[es]],
                      writes=[xdb[dc][j]], key=xnb[es])

    def conv(self, xsrc, xsb, xdst, xdb):
        p = self.p
        p.push()
        hT = p.sb("chT", [128, 16, S], BF16)
        hb = [Buf() for _ in range(16)]
        self.norm(xsrc, xsb, "g_conv", hT, hb)
        mT = p.sb("cmT", [128, 16, S], BF16)
        mb = [Buf() for _ in range(16)]
        p.push()
        w = [p.sb("cw%d" % i, [128, 2048], BF16) for i in range(6)]
        wb = [Buf() for _ in range(6)]
        cgs = p.sb("ccg", [128, S], F32)
        cgb = Buf()
        z = p.sb("cz", [128, S], F32)
        zb = Buf()
        cv = p.sb("ccv", [128, S], F32)
        cvb = Buf()
        W = self.A["cw_in"]
        for t in range(3):
            self.wload(wb[t], w[t], W[t], 2048)
        for j in range(16):
            o = (j % 2) * 3
            if j + 1 < 16:
                o1 = ((j + 1) % 2) * 3
                for t in range(3):
                    self.wload(wb[o1 + t], w[o1 + t], W[3 * (j + 1) + t], 2048)

            def proj(t, b0):
                for jj in range(4):
                    for kc in range(16):
                        p.op("pe", lambda e: e.matmul(self.bk(b0 + jj), w[o + t][:, kc * 128:(kc + 1) * 128],
                                                      hT[:, kc, 512 * jj:512 * jj + 512],
                                                      start=(kc == 0), stop=(kc == 15)),
                             reads=[wb[o + t], hb[kc]], writes=[self.bank[b0 + jj]], inc=(kc == 15))
            proj(1, 0)
            proj(2, 4)
            for jj in range(4):
                p.op("act", lambda e: e.activation(out=cgs[:, 512 * jj:512 * jj + 512], in_=self.bk(jj), func=AF.Copy),
                     reads=[self.bank[jj]], writes=[cgb])
            proj(0, 0)
            for jj in range(4):
                p.op("dve", lambda e: e.tensor_tensor(out=z[:, 512 * jj:512 * jj + 512], in0=self.bk(4 + jj),
                                                      in1=cgs[:, 512 * jj:512 * jj + 512], op=ALU.mult),
                     reads=[self.bank[4 + jj], cgb], writes=[zb])
            w0, w1, w2 = (self.vcol("cw", 3 * j + t) for t in range(3))
            p.op("dve", lambda e: e.tensor_scalar(out=cv[:], in0=z[:], scalar1=w1, scalar2=None, op0=ALU.mult),
                 reads=[zb, self.cb], writes=[cvb])
            p.op("dve", lambda e: e.scalar_tensor_tensor(out=cv[:, 1:S], in0=z[:, 0:S - 1], scalar=w0,
                                                         in1=cv[:, 1:S], op0=ALU.mult, op1=ALU.add),
                 reads=[zb, cvb], writes=[cvb])
            p.op("dve", lambda e: e.scalar_tensor_tensor(out=cv[:, 0:S - 1], in0=z[:, 1:S], scalar=w2,
                                                         in1=cv[:, 0:S - 1], op0=ALU.mult, op1=ALU.add),
                 reads=[zb, cvb], writes=[cvb])
            for jj in range(4):
                p.op("dve", lambda e: e.tensor_tensor(out=mT[:, j, 512 * jj:512 * jj + 512], in0=self.bk(jj),
                                                      in1=cv[:, 512 * jj:512 * jj + 512], op=ALU.mult),
                     reads=[self.bank[jj], cvb], writes=[mb[j]])
        p.pop()
        p.push()
        self.outproj(self.A["cw_out"], mT, mb, xsrc, xsb, xdst, xdb)
        p.pop()
        p.pop()

    def attn(self, xsrc, xsb, xdst, xdb):
        p = self.p
        nc = self.nc
        AX = mybir.AxisListType.X

        def dsc(name, shape, dt=BF16):
            return nc.dram_tensor(name, shape, dt, kind="Internal").ap()
        AQK = dsc("AQK", [24, 128, S]); bAQK = [Buf() for _ in range(24)]
        VA = dsc("VA", [8, 128, S]); bVA = [Buf() for _ in range(8)]
        QN = dsc("QN", [8, 128, S]); bQN = [Buf() for _ in range(8)]
        KN = dsc("KN", [8, 128, S]); bKN = [Buf() for _ in range(8)]
        QR = dsc("QR", [8, 128, S]); bQR = [Buf() for _ in range(8)]
        KR = dsc("KR", [8, 128, S]); bKR = [Buf() for _ in range(8)]
        VB = dsc("VB", [8, 128, S]); bVB = [Buf() for _ in range(8)]
        OT = dsc("OT", [16, 128, S]); bOT = [Buf() for _ in range(16)]
        TR = dsc("TR", [8, NR], F32); bTR = Buf()
        SCA = 64 ** -0.5
        SCB = 192 ** -0.5
        p.push()
        cs = p.sb("acs", [128, 16], F32); csb = Buf()
        far = p.sb("afar", [128, 16], F32); farb = Buf()
        p.push()
        lamt = p.sb("lamt", [128, 256], F32); lb = Buf()
        p.dma("sp", lamt[:], self.A["lamv"][0].partition_broadcast(128), writes=[lb], key=lb)
        prod = p.sb("lprod", [128, 128], F32); pb = Buf()
        red = p.sb("lred", [128, 4], F32); rb_ = Buf()
        p.op("dve", lambda e: e.tensor_tensor(out=prod[:], in0=lamt[:, 0:128], in1=lamt[:, 128:256], op=ALU.mult),
             reads=[lb], writes=[pb])
        p.op("dve", lambda e: e.tensor_reduce(out=red[:, 0:2], in_=prod[:].rearrange("p (a b) -> p a b", b=64),
                                              axis=AX, op=ALU.add), reads=[pb], writes=[rb_])
        p.op("act", lambda e: e.activation(out=red[:, 2:4], in_=red[:, 0:2], func=AF.Exp), reads=[rb_], writes=[rb_])
        p.op("dve", lambda e: e.tensor_tensor(out=cs[:, 2:3], in0=red[:, 3:4], in1=red[:, 2:3], op=ALU.subtract),
             reads=[rb_], writes=[csb])
        p.op("dve", lambda e: e.tensor_scalar(out=cs[:, 2:3], in0=cs[:, 2:3], scalar1=-LAM_INIT0, scalar2=None,
                                              op0=ALU.add), reads=[csb], writes=[csb])
        for col, nm, sc in ((0, "dq", SCA), (1, "subln", 1.0 - LAM_INIT0), (3, "mq_n", SCB), (4, "mq_r", SCB),
                            (5, "mq_s", SCB)):
            p.op("dve", lambda e: e.tensor_scalar(out=cs[:, col:col + 1], in0=self.vcol(nm), scalar1=float(sc),
                                                  scalar2=None, op0=ALU.mult), reads=[self.cb, csb], writes=[csb])
        relt = p.sb("relt", [32, 8], BF16); rlb = Buf()
        oneh = p.sb("oneh", [32, NR + 256], BF16); ohb = Buf()
        p.dma("pool", relt[:], self.A["relt"], writes=[rlb], key=rlb)
        p.dma("pool", oneh[:], self.A["oneh"], writes=[ohb], key=ohb)
        trs = p.sb("trs", [8, NR], F32); trb = Buf()
        for c in range(3):
            p.op("pe", lambda e: e.matmul(self.PS[0:8, c, :], relt[:, :], oneh[:, 512 * c:512 * c + 512],
                                          start=True, stop=True), reads=[rlb, ohb], writes=[self.bank[c]])
            p.op("act", lambda e: e.activation(out=trs[:, 512 * c:512 * c + 512], in_=self.PS[0:8, c, :], func=AF.Copy),
                 reads=[self.bank[c]], writes=[trb])
        p.dma("sp", TR, trs[:], reads=[trb], writes=[bTR], key=trb)
        for c in range(2):
            p.op("pe", lambda e: e.matmul(self.PS[:, 3 + c, 0:8], oneh[:, NR + 128 * c:NR + 128 * c + 128], relt[:, :],
                                          start=True, stop=True), reads=[rlb, ohb], writes=[self.bank[3 + c]])
            p.op("act", lambda e: e.activation(out=far[:, 8 * c:8 * c + 8], in_=self.PS[:, 3 + c, 0:8], func=AF.Copy),
                 reads=[self.bank[3 + c]], writes=[farb])
        p.pop()
        STG = int(os.environ.get("ATT_STAGE", "9"))
        if STG < 1:
            p.pop()
            return
        p.push()
        hT = p.sb("ahT", [128, 16, S], BF16)
        hb = [Buf() for _ in range(16)]
        self.norm(xsrc, xsb, "g_attn", hT, hb)
        Ct = p.sb("ropeC", [128, S], F32); Ctb = Buf()
        St = p.sb("ropeS", [128, S], F32); Stb = Buf()
        p.push()
        posi = p.sb("posi", [128, S], I32); pib = Buf()
        p.dma("sp", posi[:], self.pos_d[0].partition_broadcast(128), writes=[pib], key=pib)
        posf = p.sb("posf", [128, S], F32); pfb = Buf()
        p.op("dve", lambda e: e.tensor_copy(out=posf[:], in_=posi[:]), reads=[pib], writes=[pfb])
        tt_ = p.sb("rt", [128, S], F32); tb_ = Buf()
        ti = p.sb("rti", [128, S], I32); tib = Buf()
        tf = p.sb("rtf", [128, S], F32); tfb = Buf()
        for tab, tabb, addq in ((St, Stb, 0.0), (Ct, Ctb, 0.25)):
            p.op("dve", lambda e: e.tensor_scalar(out=tt_[:], in0=posf[:], scalar1=self.vcol("invf"),
                                                  scalar2=float(addq), op0=ALU.mult, op1=ALU.add),
                 reads=[pfb, self.cb], writes=[tb_])
            p.op("dve", lambda e: e.tensor_copy(out=ti[:], in_=tt_[:]), reads=[tb_], writes=[tib])
            p.op("dve", lambda e: e.tensor_copy(out=tf[:], in_=ti[:]), reads=[tib], writes=[tfb])
            p.op("dve", lambda e: e.tensor_tensor(out=tt_[:], in0=tt_[:], in1=tf[:], op=ALU.subtract),
                 reads=[tfb, tb_], writes=[tb_])
            p.op("dve", lambda e: e.tensor_scalar(out=tf[:], in0=tt_[:], scalar1=0.5, scalar2=None, op0=ALU.is_gt),
                 reads=[tb_], writes=[tfb])
            p.op("dve", lambda e: e.tensor_tensor(out=tt_[:], in0=tt_[:], in1=tf[:], op=ALU.subtract),
                 reads=[tfb, tb_], writes=[tb_])
            p.op("dve", lambda e: e.tensor_scalar(out=tf[:], in0=tt_[:], scalar1=-0.5, scalar2=None, op0=ALU.is_lt),
                 reads=[tb_], writes=[tfb])
            p.op("dve", lambda e: e.tensor_tensor(out=tt_[:], in0=tt_[:], in1=tf[:], op=ALU.add),
                 reads=[tfb, tb_], writes=[tb_])
            p.op("act", lambda e: e.activation(out=tab[:], in_=tt_[:], func=AF.Sin,
                                               scale=float(2 * math.pi * (1 - 1e-6))),
                 reads=[tb_], writes=[tabb])
        p.op("dve", lambda e: e.tensor_scalar(out=St[:], in0=St[:], scalar1=self.vcol("sgn"), scalar2=None,
                                              op0=ALU.mult), reads=[Stb, self.cb], writes=[Stb])
        p.pop()
        if STG < 2:
            p.pop()
            p.pop()
            return
        cqn = p.sb("cqn", [128, 4, S], BF16); cqb = [Buf() for _ in range(4)]
        ckn = p.sb("ckn", [128, 2, S], BF16); ckb = [Buf() for _ in range(2)]
        sqkr = p.sb("sqkr", [128, S], BF16); sqkb = Buf()
        krbase = p.sb("krbase", [128, S], F32); krb = Buf()
        NW = 3
        ws = [p.sb("aw%d" % i, [128, 2048], BF16) for i in range(NW)]
        wsb = [Buf() for _ in range(NW)]
        osl = [p.sb("aos%d" % i, [128, S], BF16) for i in range(3)]
        oslb = [Buf() for _ in range(3)]
        orl = [p.sb("aor%d" % i, [128, S], BF16) for i in range(2)]
        orlb = [Buf() for _ in range(2)]
        sq = [p.sb("asq%d" % i, [128, 512], BF16) for i in range(2)]
        sqb = [Buf() for _ in range(2)]
        sq2 = [p.sb("asr%d" % i, [128, 512], BF16) for i in range(2)]
        sq2b = [Buf() for _ in range(2)]
        rr = [p.sb("arr%d" % i, [128, 512], F32) for i in range(2)]
        rrb = [Buf() for _ in range(2)]
        raw = p.sb("araw", [128, 4, 512], F32); rawb = [Buf() for _ in range(4)]
        t1 = p.sb("at1", [128, 512], F32); t1b = Buf()
        t2 = p.sb("at2", [128, 512], F32); t2b = Buf()
        st = {"w": 0, "o": 0, "t": 0, "bk": 0}

        def nextw(src, width):
            i = st["w"] % NW
            st["w"] += 1
            self.wload(wsb[i], ws[i], src, width)
            return ws[i], wsb[i]

        def nbank():
            b = st["bk"] % 6
            st["bk"] += 1
            return b

        def proj(w, wb_, KC, M, src, srcb, j, b, rows=None):
            for kc in range(KC):
                p.op("pe", lambda e: e.matmul(self.PS[0:M, b, :], w[:, kc * M:(kc + 1) * M],
                                              src[:, kc, 512 * j:512 * j + 512], start=(kc == 0), stop=(kc == KC - 1)),
                     reads=[wb_, srcb[kc]], writes=[self.bank[b]], inc=(kc == KC - 1))

        def rstd_from(bB, nfeat, k):
            p.op("act", lambda e: e.activation(out=rr[k][:], in_=self.bk(bB), func=AF.Sqrt, scale=1.0 / nfeat,
                                               bias=self.eps_ap()), reads=[self.bank[bB]], writes=[rrb[k]])
            p.op("dve", lambda e: e.reciprocal(out=rr[k][:], in_=rr[k][:]), reads=[rrb[k]], writes=[rrb[k]])

        W = self.A
        SUB = int(os.environ.get("ATT_SUB", "9"))

        def bail():
            p.pop()
            p.pop()
        for c in range(16):
            w, wb_ = nextw(W["w_in_st"][c], 2048)
            o = 0
            gcol = cs[:, 0:1] if c < 8 else self.vcol("dk_lo")
            for j in range(4):
                k = st["t"] % 2
                st["t"] += 1
                bA, bB = nbank(), 6 + k
                proj(w, wb_, 16, 128, hT, hb, j, bA)
                p.op("act", lambda e: e.activation(out=sq[k][:], in_=self.bk(bA), func=AF.Square),
                     reads=[self.bank[bA]], writes=[sqb[k]])
                p.op("pe", lambda e: e.matmul(self.bk(bB), self.blk, sq[k][:], start=True, stop=True),
                     reads=[sqb[k], self.cb2], writes=[self.bank[bB]])
                rstd_from(bB, 64, k)
                p.op("dve", lambda e: e.scalar_tensor_tensor(out=osl[o][:, 512 * j:512 * j + 512], in0=self.bk(bA),
                                                             scalar=gcol, in1=rr[k][:], op0=ALU.mult, op1=ALU.mult),
                     reads=[self.bank[bA], rrb[k], csb, self.cb], writes=[oslb[o]])
                if c >= 8:
                    p.op("dve", lambda e: e.scalar_tensor_tensor(out=osl[2][:, 512 * j:512 * j + 512], in0=self.bk(bA),
                                                                 scalar=self.vcol("dk_hi"), in1=rr[k][:], op0=ALU.mult,
                                                                 op1=ALU.mult),
                         reads=[self.bank[bA], rrb[k], self.cb], writes=[oslb[2]])
            p.dma("sp", AQK[c], osl[o][:], reads=[oslb[o]], writes=[bAQK[c]], key=oslb[o])
            if c >= 8:
                p.dma("sp", AQK[c + 8], osl[2][:], reads=[oslb[2]], writes=[bAQK[c + 8]], key=oslb[2])

        def vproj(wsrc, KC, src, srcb, dst, dstb):
            for h in range(8):
                w, wb_ = nextw(wsrc[h], KC * 128)
                o = st["o"] % 2
                st["o"] += 1
                for q4 in range(4):
                    b = nbank()
                    for t4 in range(4):
                        tt = q4 * 4 + t4
                        for kc in range(KC):
                            p.op("pe", lambda e: e.matmul(self.PS[:, b, 128 * t4:128 * t4 + 128],
                                                          src[:, kc, 128 * tt:128 * tt + 128],
                                                          w[:, kc * 128:(kc + 1) * 128],
                                                          start=(kc == 0), stop=(kc == KC - 1)),
                                 reads=[wb_, srcb[kc]], writes=[self.bank[b]], inc=(kc == KC - 1 and t4 == 3))
                    p.op("act", lambda e: e.activation(out=osl[o][:, 512 * q4:512 * q4 + 512], in_=self.bk(b),
                                                       func=AF.Copy), reads=[self.bank[b]], writes=[oslb[o]])
                p.dma("sp", dst[h], osl[o][:], reads=[oslb[o]], writes=[dstb[h]], key=oslb[o])
        if SUB < 2:
            return bail()
        vproj(W["w_in_v"], 16, hT, hb, VA, bVA)
        if SUB < 3:
            return bail()

        for base, nchk, dstT, dstb_, gname in ((16, 4, cqn, cqb, "qa"), (20, 2, ckn, ckb, "kva")):
            wl = [nextw(W["w_in_st"][base + c], 2048) for c in range(nchk)] if nchk <= NW else None
            for j in range(4):
                k = st["t"] % 2
                st["t"] += 1
                bB = 6 + k
                for c in range(nchk):
                    if wl is None:
                        w, wb_ = nextw(W["w_in_st"][base + c], 2048)
                    else:
                        w, wb_ = wl[c]
                    bA = nbank()
                    proj(w, wb_, 16, 128, hT, hb, j, bA)
                    p.op("act", lambda e: e.activation(out=raw[:, c, :], in_=self.bk(bA), func=AF.Copy),
                         reads=[self.bank[bA]], writes=[rawb[c]])
                    p.op("act", lambda e: e.activation(out=sq[c % 2][:], in_=self.bk(bA), func=AF.Square),
                         reads=[self.bank[bA]], writes=[sqb[c % 2]])
                    p.op("pe", lambda e: e.matmul(self.bk(bB), self.ones, sq[c % 2][:], start=(c == 0),
                                                  stop=(c == nchk - 1)),
                         reads=[sqb[c % 2], self.cb2], writes=[self.bank[bB]])
                rstd_from(bB, 128 * nchk, k)
                for c in range(nchk):
                    p.op("dve", lambda e: e.scalar_tensor_tensor(out=dstT[:, c, 512 * j:512 * j + 512], in0=raw[:, c, :],
                                                                 scalar=self.vcol(gname, c), in1=rr[k][:],
                                                                 op0=ALU.mult, op1=ALU.mult),
                         reads=[rawb[c], rrb[k], self.cb], writes=[dstb_[c]])

        def rope_mix(bR, bS, gr, gs, j, outap, outb, extra_reads=()):
            p.op("dve", lambda e: e.scalar_tensor_tensor(out=t1[:], in0=self.bk(bR), scalar=gr,
                                                         in1=Ct[:, 512 * j:512 * j + 512], op0=ALU.mult, op1=ALU.mult),
                 reads=[self.bank[bR], Ctb, csb, self.cb], writes=[t1b])
            p.op("dve", lambda e: e.scalar_tensor_tensor(out=t2[:], in0=self.bk(bS), scalar=gs,
                                                         in1=St[:, 512 * j:512 * j + 512], op0=ALU.mult, op1=ALU.mult),
                 reads=[self.bank[bS], Stb, csb, self.cb], writes=[t2b])
            p.op("dve", lambda e: e.tensor_tensor(out=outap, in0=t1[:], in1=t2[:], op=ALU.add),
                 reads=[t1b, t2b] + list(extra_reads), writes=[outb])

        if SUB < 4:
            return bail()
        wkr, wkrb = nextw(W["w_in_kr"][0], 2048)
        wks, wksb = nextw(W["w_in_kr"][1], 2048)
        for j in range(4):
            bA, bS = nbank(), nbank()
            proj(wkr, wkrb, 16, 128, hT, hb, j, bA)
            proj(wks, wksb, 16, 128, hT, hb, j, bS)
            p.op("act", lambda e: e.activation(out=sqkr[:, 512 * j:512 * j + 512], in_=self.bk(bA),
                                               func=AF.Square), reads=[self.bank[bA]], writes=[sqkb])
            if os.environ.get("NOROPE") != "1":
                rope_mix(bA, bS, self.vcol("mk_r"), self.vcol("mk_s"), j,
                         krbase[:, 512 * j:512 * j + 512], krb)

        if SUB < 5:
            return bail()
        for h in range(8):
            wn, wnb = nextw(W["w_uq_n"][h], 512)
            wr, wrb = nextw(W["w_uq_r"][2 * h], 512)
            wsw, wswb = nextw(W["w_uq_r"][2 * h + 1], 512)
            o = st["o"] % 2
            st["o"] += 1
            for j in range(4):
                k = st["t"] % 2
                st["t"] += 1
                bB = 6 + k
                bA, bR, bS = nbank(), nbank(), nbank()
                proj(wn, wnb, 4, 128, cqn, cqb, j, bA)
                proj(wr, wrb, 4, 128, cqn, cqb, j, bR)
                proj(wsw, wswb, 4, 128, cqn, cqb, j, bS)
                p.op("act", lambda e: e.activation(out=sq[k][:], in_=self.bk(bA), func=AF.Square),
                     reads=[self.bank[bA]], writes=[sqb[k]])
                p.op("act", lambda e: e.activation(out=sq2[k][:], in_=self.bk(bR), func=AF.Square),
                     reads=[self.bank[bR]], writes=[sq2b[k]])
                p.op("pe", lambda e: e.matmul(self.bk(bB), self.ones, sq[k][:], start=True, stop=False),
                     reads=[sqb[k], self.cb2], writes=[self.bank[bB]], inc=False)
                p.op("pe", lambda e: e.matmul(self.bk(bB), self.oneslo, sq2[k][:], start=False, stop=True),
                     reads=[sq2b[k], self.cb2], writes=[self.bank[bB]])
                rstd_from(bB, 192, k)
                p.op("dve", lambda e: e.scalar_tensor_tensor(out=osl[o][:, 512 * j:512 * j + 512], in0=self.bk(bA),
                                                             scalar=cs[:, 3:4], in1=rr[k][:], op0=ALU.mult,
                                                             op1=ALU.mult),
                     reads=[self.bank[bA], rrb[k], csb], writes=[oslb[o]])
                rope_mix(bR, bS, cs[:, 4:5], cs[:, 5:6], j, t1[:], t1b)
                p.op("dve", lambda e: e.tensor_tensor(out=orl[o][:, 512 * j:512 * j + 512], in0=t1[:],
                                                      in1=rr[k][:], op=ALU.mult),
                     reads=[t1b, rrb[k]], writes=[orlb[o]])
            p.dma("sp", QN[h], osl[o][:], reads=[oslb[o]], writes=[bQN[h]], key=oslb[o])
            p.dma("sp", QR[h], orl[o][:], reads=[orlb[o]], writes=[bQR[h]], key=orlb[o])
        if SUB < 6:
            return bail()
        for h in range(8):
            wn, wnb = nextw(W["w_ukv_k"][h], 256)
            o = st["o"] % 2
            st["o"] += 1
            for j in range(4):
                k = st["t"] % 2
                st["t"] += 1
                bB = 6 + k
                bA = nbank()
                proj(wn, wnb, 2, 128, ckn, ckb, j, bA)
                p.op("act", lambda e: e.activation(out=sq[k][:], in_=self.bk(bA), func=AF.Square),
                     reads=[self.bank[bA]], writes=[sqb[k]])
                p.op("pe", lambda e: e.matmul(self.bk(bB), self.ones, sq[k][:], start=True, stop=False),
                     reads=[sqb[k], self.cb2], writes=[self.bank[bB]], inc=False)
                p.op("pe", lambda e: e.matmul(self.bk(bB), self.oneslo, sqkr[:, 512 * j:512 * j + 512],
                                              start=False, stop=True),
                     reads=[sqkb, self.cb2], writes=[self.bank[bB]])
                rstd_from(bB, 192, k)
                p.op("dve", lambda e: e.scalar_tensor_tensor(out=osl[o][:, 512 * j:512 * j + 512], in0=self.bk(bA),
                                                             scalar=self.vcol("mk_n"), in1=rr[k][:], op0=ALU.mult,
                                                             op1=ALU.mult),
                     reads=[self.bank[bA], rrb[k], self.cb], writes=[oslb[o]])
                p.op("dve", lambda e: e.tensor_tensor(out=orl[o][:, 512 * j:512 * j + 512],
                                                      in0=krbase[:, 512 * j:512 * j + 512], in1=rr[k][:],
                                                      op=ALU.mult),
                     reads=[krb, rrb[k]], writes=[orlb[o]])
            p.dma("sp", KN[h], osl[o][:], reads=[oslb[o]], writes=[bKN[h]], key=oslb[o])
            p.dma("sp", KR[h], orl[o][:], reads=[orlb[o]], writes=[bKR[h]], key=orlb[o])
        vproj(W["w_ukv_v"], 2, ckn, ckb, VB, bVB)
        p.pop()
        if STG < 3:
            p.pop()
            return

        p.push()
        NH = 2
        qs = [p.sb("uq%d" % i, [128, S], BF16) for i in range(NH)]; qsb = [Buf() for _ in range(NH)]
        ks = [p.sb("uk%d" % i, [128, S], BF16) for i in range(NH)]; ksb = [Buf() for _ in range(NH)]
        vs = [p.sb("uv%d" % i, [128, 16, 128], BF16) for i in range(NH)]; vsb = [Buf() for _ in range(NH)]
        qrs = [p.sb("uqr%d" % i, [128, S], BF16) for i in range(NH)]; qrsb = [Buf() for _ in range(NH)]
        krs = [p.sb("ukr%d" % i, [128, S], BF16) for i in range(NH)]; krsb = [Buf() for _ in range(NH)]
        gb = [p.sb("ug%d" % i, [128, 6, 512], BF16) for i in range(NH)]; gbb = [Buf() for _ in range(NH)]
        NP = 4
        pT = [p.sb("upT%d" % i, [128, 512], BF16) for i in range(NP)]; pTb = [Buf() for _ in range(NP)]
        on = [p.sb("uon%d" % i, [128, S], F32) for i in range(2)]; onb = [Buf() for _ in range(2)]
        rs = [p.sb("urs%d" % i, [128, 512], F32) for i in range(2)]; rsb = [Buf() for _ in range(2)]
        dd = p.sb("udd", [128, S], F32); ddb = Buf()
        dsq = [p.sb("udsq%d" % i, [128, 512], BF16) for i in range(2)]; dsqb = [Buf() for _ in range(2)]
        drr = [p.sb("udr%d" % i, [128, 512], F32) for i in range(2)]; drrb = [Buf() for _ in range(2)]
        oo = [p.sb("uoo%d" % i, [128, S], BF16) for i in range(2)]; oob = [Buf() for _ in range(2)]
        us = {"s": 0, "p": 0, "q": 0, "o": 0, "d": 0}

        def unit(kind, h, half, sl, oni):
            r0, nr = 0, 128
            kk, kkb = (krs, krsb) if (kind == "a" and half == 1) else (ks, ksb)
            for qt in range(4):
                qi = us["q"] % 2
                us["q"] += 1
                bO, bS = 3 + qi, 5 + qi

                def smm(kt):
                    b = us["s"] % 3
                    us["s"] += 1
                    dl = kt - 4 * qt
                    near = (kind == "a") and (-1 <= dl <= 4)
                    last = "b" if kind == "b" else ("n" if near else "m")
                    p.op("pe", lambda e: e.matmul(self.bk(b), kk[sl][r0:r0 + nr, 128 * kt:128 * kt + 128],
                                                  qs[sl][r0:r0 + nr, 512 * qt:512 * qt + 512], start=True,
                                                  stop=(last == "m")),
                         reads=[kkb[sl], qsb[sl]], writes=[self.bank[b]], inc=(last == "m"))
                    if kind == "b":
                        p.op("pe", lambda e: e.matmul(self.bk(b), krs[sl][:, 128 * kt:128 * kt + 128],
                                                      qrs[sl][:, 512 * qt:512 * qt + 512], start=False, stop=True),
                             reads=[krsb[sl], qrsb[sl]], writes=[self.bank[b]])
                    elif near:
                        p.op("pe", lambda e: e.matmul(self.bk(b), self.anti, gb[sl][:, dl + 1, :], start=False,
                                                      stop=True),
                             reads=[gbb[sl], self.cb2], writes=[self.bank[b]])
                    return b, (None if (kind == "b" or near) else (0 if dl < 0 else 1))
                nxt = smm(0)
                for kt in range(16):
                    b, fr = nxt
                    if kt + 1 < 16:
                        nxt = smm(kt + 1)
                    pi = us["p"] % NP
                    us["p"] += 1
                    if fr is None:
                        p.op("act", lambda e: e.activation(out=pT[pi][:], in_=self.bk(b), func=AF.Exp),
                             reads=[self.bank[b]], writes=[pTb[pi]])
                    else:
                        p.op("act", lambda e: e.activation(out=pT[pi][:], in_=self.bk(b), func=AF.Exp,
                                                           bias=far[:, 8 * fr + h:8 * fr + h + 1]),
                             reads=[self.bank[b], farb], writes=[pTb[pi]])
                    p.op("pe", lambda e: e.matmul(self.bk(bO), vs[sl][:, kt, :], pT[pi][:], start=(kt == 0),
                                                  stop=(kt == 15)),
                         reads=[vsb[sl], pTb[pi]], writes=[self.bank[bO]], inc=False)
                    p.op("pe", lambda e: e.matmul(self.bk(bS), self.ones, pT[pi][:], start=(kt == 0), stop=(kt == 15)),
                         reads=[pTb[pi], self.cb2], writes=[self.bank[bS]])
                p.op("dve", lambda e: e.reciprocal(out=rs[qi][:], in_=self.bk(bS)), reads=[self.bank[bS]],
                     writes=[rsb[qi]])
                p.op("dve", lambda e: e.tensor_tensor(out=on[oni][:, 512 * qt:512 * qt + 512], in0=self.bk(bO),
                                                      in1=rs[qi][:], op=ALU.mult),
                     reads=[self.bank[bO], rsb[qi]], writes=[onb[oni]])

        def vsrc(T, h):
            return T[h].rearrange("p (t d) -> p t d", d=128)

        def load_a(h, sl):
            p.dma("sp", qs[sl][:], AQK[h], reads=[bAQK[h]], writes=[qsb[sl]], key=qsb[sl])
            p.dma("sp", ks[sl][:], AQK[8 + h], reads=[bAQK[8 + h]], writes=[ksb[sl]], key=ksb[sl])
            p.dma("sp", krs[sl][:], AQK[16 + h], reads=[bAQK[16 + h]], writes=[krsb[sl]], key=krsb[sl])
            p.dma("sp", vs[sl][:], vsrc(VA, h), reads=[bVA[h]], writes=[vsb[sl]], key=vsb[sl])
            for dl in range(-1, 5):
                src = bass.AP(tensor=TR.tensor, offset=h * NR + 640 - 128 * dl, ap=[[1, 128], [1, 512]])
                p.dma("pool", gb[sl][:, dl + 1, :], src, reads=[bTR], writes=[gbb[sl]], key=gbb[sl])

        def load_b(h, sl):
            p.dma("sp", qs[sl][:], QN[h], reads=[bQN[h]], writes=[qsb[sl]], key=qsb[sl])
            p.dma("sp", ks[sl][:], KN[h], reads=[bKN[h]], writes=[ksb[sl]], key=ksb[sl])
            p.dma("sp", vs[sl][:], vsrc(VB, h), reads=[bVB[h]], writes=[vsb[sl]], key=vsb[sl])
            p.dma("sp", qrs[sl][:], QR[h], reads=[bQR[h]], writes=[qrsb[sl]], key=qrsb[sl])
            p.dma("sp", krs[sl][:], KR[h], reads=[bKR[h]], writes=[krsb[sl]], key=krsb[sl])

        heads = [("a", h) for h in range(8)] + [("b", h) for h in range(8)]
        (load_a if heads[0][0] == "a" else load_b)(heads[0][1], 0)
        for i, (kind, h) in enumerate(heads):
            sl = i % NH
            if i + 1 < len(heads):
                (load_a if heads[i + 1][0] == "a" else load_b)(heads[i + 1][1], (i + 1) % NH)
            o = us["o"] % 2
            us["o"] += 1
            if kind == "a":
                unit("a", h, 0, sl, 0)
                unit("a", h, 1, sl, 1)
                p.op("dve", lambda e: e.scalar_tensor_tensor(out=dd[:], in0=on[1][:], scalar=cs[:, 2:3], in1=on[0][:],
                                                             op0=ALU.mult, op1=ALU.add),
                     reads=[onb[0], onb[1], csb], writes=[ddb])
                for j in range(4):
                    k = us["d"] % 2
                    us["d"] += 1
                    bB = 6 + k if False else 7
                    p.op("act", lambda e: e.activation(out=dsq[k][:], in_=dd[:, 512 * j:512 * j + 512], func=AF.Square),
                         reads=[ddb], writes=[dsqb[k]])
                    p.op("pe", lambda e: e.matmul(self.bk(bB), self.ones, dsq[k][:], start=True, stop=True),
                         reads=[dsqb[k], self.cb2], writes=[self.bank[bB]])
                    p.op("act", lambda e: e.activation(out=drr[k][:], in_=self.bk(bB), func=AF.Sqrt, scale=1.0 / 128,
                                                       bias=self.eps_ap()), reads=[self.bank[bB]], writes=[drrb[k]])
                    p.op("dve", lambda e: e.reciprocal(out=drr[k][:], in_=drr[k][:]), reads=[drrb[k]], writes=[drrb[k]])
                    p.op("dve", lambda e: e.scalar_tensor_tensor(out=oo[o][:, 512 * j:512 * j + 512],
                                                                 in0=dd[:, 512 * j:512 * j + 512], scalar=cs[:, 1:2],
                                                                 in1=drr[k][:], op0=ALU.mult, op1=ALU.mult),
                         reads=[ddb, drrb[k], csb], writes=[oob[o]])
                p.dma("sp", OT[h], oo[o][:], reads=[oob[o]], writes=[bOT[h]], key=oob[o])
            else:
                unit("b", h, 0, sl, 0)
                p.op("act", lambda e: e.activation(out=oo[o][:], in_=on[0][:], func=AF.Copy),
                     reads=[onb[0]], writes=[oob[o]])
                p.dma("sp", OT[8 + h], oo[o][:], reads=[oob[o]], writes=[bOT[8 + h]], key=oob[o])
        p.pop()

        if STG < 4:
            p.pop()
            return
        p.push()
        oT = p.sb("aoT", [128, 16, S], BF16)
        ob = [Buf() for _ in range(16)]
        for c in range(16):
            p.dma("sp", oT[:, c, :], OT[c], reads=[bOT[c]], writes=[ob[c]], key=ob[c])
        self.outproj(self.A["w_out"], oT, ob, xsrc, xsb, xdst, xdb)
        p.pop()
        p.pop()


def xbufs():
    return [[Buf() for _ in range(4)] for _ in range(16)]


def flat(bl):
    return bl


def build(phases):
    k = K(phases)
    p = k.p
    k.eps_ap()
    cur, curb = k.x_in, xbufs()
    seq = [ph for ph in ("attn", "ffn0", "conv", "ffn1") if ph in phases]
    scratch = list(k.xs)
    for i, ph in enumerate(seq):
        last = (i == len(seq) - 1)
        dst = k.x_out if last else scratch.pop(0)
        dstb = xbufs()
        if ph == "ffn0":
            k.ffn(0, cur, curb, dst, dstb)
        elif ph == "ffn1":
            k.ffn(1, cur, curb, dst, dstb)
        elif ph == "conv":
            k.conv(cur, curb, dst, dstb)
        elif ph == "attn":
            k.attn(cur, curb, dst, dstb)
        cur, curb = dst, dstb
    p.finish([b for row in curb for b in row], "sp")
    p.es.close()
    return k


_CACHE = {}


def run(I, phases=("attn", "ffn0", "conv", "ffn1"), cores=8, xT_override=None):
    sh = prep_shared(I)
    x = np.asarray(I["x"], dtype=np.float32)
    pos = np.asarray(I["positions"]).astype(np.int32)
    key = tuple(phases)
    if key not in _CACHE:
        _CACHE[key] = build(phases)
    k = _CACHE[key]
    in_maps = []
    for c in range(cores):
        m = dict(sh)
        xt = x[c].T if xT_override is None else xT_override[c]
        m["xT"] = np.ascontiguousarray(xt).reshape(16, 128, S)
        m["pos"] = pos[c][None, :].copy()
        in_maps.append(m)
    res = run_bass_kernel_spmd(k.nc, in_maps, core_ids=list(range(cores)))
    outs = [np.ascontiguousarray(r["yT"].reshape(D, S).T) for r in res.results]
    return np.stack(outs, 0)


def kernel(**inputs):
    return run(inputs).astype(np.float32)
```

```python
from contextlib import ExitStack
import math
import os
import numpy as np
import concourse.bass as bass
import concourse.mybir as mybir
from concourse.bass_utils import run_bass_kernel_spmd

F32 = mybir.dt.float32
BF16 = mybir.dt.bfloat16
I32 = mybir.dt.int32
ALU = mybir.AluOpType
AF = mybir.ActivationFunctionType

D = 2048
S = 2048
DFF = 5632
NCH = 44
GRP = 11
EPS = 1e-6
NR = 1536
LAM_INIT0 = 0.8 - 0.6 * math.exp(-0.3 * 0)


class Buf:
    __slots__ = ("w", "r", "sem", "scnt", "excl", "cls")

    def __init__(self, excl=False):
        self.excl = excl
        self.w = None
        self.r = {}
        self.sem = None
        self.scnt = 0


class Prog:
    def __init__(self):
        self.nc = bass.Bass("TRN2", target_bir_lowering=False)
        self.es = ExitStack()
        nc = self.nc
        self.eng = dict(pe=nc.tensor, act=nc.scalar, dve=nc.vector, pool=nc.gpsimd, sp=nc.sync)
        self.esem = {k: self.es.enter_context(nc.semaphore("sem_" + k)) for k in self.eng}
        self.ecnt = {k: 0 for k in self.eng}
        self.seen = {k: {} for k in self.eng}
        self.dsems = []
        self.free_sems = {"sw": [], "hw": []}
        self.skeys = [[]]
        self.nds = 0
        self.scope = None

    def sb(self, name, shape, dt):
        st = self.scope if self.scope is not None else self.es
        self.nsb = getattr(self, "nsb", 0) + 1
        return st.enter_context(self.nc.sbuf_tensor("%s_%d" % (name, self.nsb), list(shape), dt))

    def push(self):
        self.scopes = getattr(self, "scopes", [])
        self.scopes.append(self.scope)
        self.scope = ExitStack()
        self.skeys.append([])

    def pop(self):
        self.barrier()
        self.scope.close()
        self.scope = self.scopes.pop()
        for b in self.skeys.pop():
            self.free_sems[b.cls].append((b.sem, b.scnt))
            self.dsems.remove(b)
            b.sem = None

    def ps(self, name, shape, dt):
        return self.es.enter_context(self.nc.psum_tensor(name, list(shape), dt))

    def _wait(self, e, toks):
        need = {}
        seen = self.seen[e]
        for t in toks:
            if t is None:
                continue
            s, v = t
            k = id(s)
            if seen.get(k, 0) >= v:
                continue
            if k not in need or need[k][1] < v:
                need[k] = (s, v)
        for k, (s, v) in need.items():
            self.eng[e].wait_ge(s, v)
            seen[k] = v

    def _deps(self, reads, writes):
        toks = []
        for b in reads:
            toks.append(b.w)
            if b.excl:
                toks.extend(b.r.values())
        for b in writes:
            toks.append(b.w)
            toks.extend(b.r.values())
        return toks

    def _commit(self, tok, reads, writes):
        s, v = tok
        k = id(s)
        for b in reads:
            o = b.r.get(k)
            if o is None or o[1] < v:
                b.r[k] = tok
        for b in writes:
            b.w = tok
            b.r = {}

    def op(self, e, fn, reads=(), writes=(), inc=True):
        toks = self._deps(reads, writes)
        if e == "pe":
            pes = id(self.esem["pe"])
            toks = [t for t in toks if t is not None and id(t[0]) != pes]
        self._wait(e, toks)
        ins = fn(self.eng[e])
        if inc:
            self.ecnt[e] += 1
            ins.then_inc(self.esem[e], 1)
            tok = (self.esem[e], self.ecnt[e])
        else:
            tok = (self.esem[e], self.ecnt[e] + 1)
        self._commit(tok, reads, writes)
        return tok

    def dma(self, q, out, in_, reads=(), writes=(), key=None, **kw):
        toks = self._deps(reads, writes)
        self._wait(q, toks)
        if key.sem is None:
            key.cls = "sw" if q == "pool" else "hw"
            if self.free_sems[key.cls]:
                key.sem, key.scnt = self.free_sems[key.cls].pop()
            else:
                key.sem = self.es.enter_context(self.nc.semaphore("dsem%d" % self.nds))
                self.nds += 1
            self.dsems.append(key)
            self.skeys[-1].append(key)
        assert key.cls == ("sw" if q == "pool" else "hw")
        key.scnt += 16
        self.eng[q].dma_start(out=out, in_=in_, **kw).then_inc(key.sem, 16)
        tok = (key.sem, key.scnt)
        self._commit(tok, reads, writes)
        return tok

    def barrier(self):
        toks = [(self.esem[k], self.ecnt[k]) for k in self.eng if self.ecnt[k] > 0]
        toks += [(b.sem, b.scnt) for b in self.dsems]
        for e in self.eng:
            mine = id(self.esem[e])
            self._wait(e, [t for t in toks if id(t[0]) != mine])

    def finish(self, bufs, e="sp"):
        self._wait(e, [b.w for b in bufs])


def _st(W, cols, M):
    K = W.shape[0]
    KC = K // 128
    cols = np.asarray(cols)
    n = len(cols) // M
    Wc = W[:, cols].reshape(KC, 128, n, M)
    return np.ascontiguousarray(Wc.transpose(2, 1, 0, 3)).reshape(n, 128, KC * M)


def _pk(v):
    return np.ascontiguousarray(np.asarray(v).reshape(-1, 128).T)


def t5_bucket_np(rel):
    nb = 16
    me = 8
    bucket = np.where(rel > 0, nb, 0).astype(np.int32)
    n = np.abs(rel)
    nf = np.maximum(n, me).astype(np.float32)
    large = me + (np.log(nf / me) / math.log(128 / me) * (nb - me)).astype(np.int32)
    large = np.minimum(large, nb - 1)
    return bucket + np.where(n < me, n, large)


VC = {}
_c = 0
for _n, _w in [("g_attn", 16), ("g_ffn0", 16), ("g_conv", 16), ("g_ffn1", 16), ("dq", 1), ("dk", 1),
               ("subln", 1), ("qa", 4), ("kva", 2), ("mq_n", 1), ("mq_r", 1), ("mq_s", 1),
               ("mk_n", 1), ("mk_r", 1), ("mk_s", 1), ("dk_lo", 1), ("dk_hi", 1), ("cw", 48), ("fw0", 132), ("fw1", 132),
               ("fb0", 44), ("fb1", 44), ("invf", 1), ("sgn", 1), ("quarter", 1)]:
    VC[_n] = _c
    _c += _w
NV = _c


def prep_shared(I):
    f = lambda k: np.asarray(I[k], dtype=np.float32)
    sh = {}
    vec = np.zeros((128, NV), np.float32)
    vec[:, VC["g_attn"]:VC["g_attn"] + 16] = _pk(f("attn_norm_g")[0])
    vec[:, VC["g_ffn0"]:VC["g_ffn0"] + 16] = _pk(f("ffn_norm_g")[0])
    vec[:, VC["g_conv"]:VC["g_conv"] + 16] = _pk(f("conv_norm_g")[0])
    vec[:, VC["g_ffn1"]:VC["g_ffn1"] + 16] = _pk(f("ffn_norm_g")[1])
    p = np.arange(128)
    vec[:, VC["dq"]] = f("diff_q_norm_g")[0][p % 64]
    vec[:, VC["dk"]] = f("diff_k_norm_g")[0][p % 64]
    vec[0:64, VC["dk_lo"]] = f("diff_k_norm_g")[0]
    vec[64:128, VC["dk_hi"]] = f("diff_k_norm_g")[0]
    vec[:, VC["subln"]] = f("diff_subln_g")[0]
    vec[:, VC["qa"]:VC["qa"] + 4] = _pk(f("mla_q_a_norm_g")[0])
    vec[:, VC["kva"]:VC["kva"] + 2] = _pk(f("mla_kv_a_norm_g")[0])
    for nm, key in (("mq", "mla_q_norm_g"), ("mk", "mla_k_norm_g")):
        g = f(key)[0]
        vec[:, VC[nm + "_n"]] = g[:128]
        vec[:, VC[nm + "_r"]] = g[128 + (p % 64)]
        vec[:, VC[nm + "_s"]] = g[128 + ((p % 64) + 32) % 64]
        if nm == "mk":
            vec[64:128, VC["mk_r"]] = 0.0
            vec[64:128, VC["mk_s"]] = 0.0
    vec[:, VC["cw"]:VC["cw"] + 48] = f("conv_w")[0].T.reshape(16, 128, 3).transpose(1, 0, 2).reshape(128, 48)
    for l in range(2):
        vec[:, VC["fw%d" % l]:VC["fw%d" % l] + 132] = (
            f("ffn_dwconv_w")[l].T.reshape(NCH, 128, 3).transpose(1, 0, 2).reshape(128, 132))
        vec[:, VC["fb%d" % l]:VC["fb%d" % l] + 44] = _pk(f("ffn_dwconv_b")[l])
    invf = 1.0 / (10000.0 ** (np.arange(0, 64, 2, dtype=np.float32) / 64))
    vec[:, VC["invf"]] = (invf / np.float32(2 * math.pi))[p % 32]
    vec[:, VC["sgn"]] = np.where((p % 64) < 32, -1.0, 1.0)
    vec[:, VC["quarter"]] = 0.25
    sh["vec"] = vec
    sh["lamv"] = np.concatenate([f("diff_lambda_q1")[0], f("diff_lambda_q2")[0],
                                 f("diff_lambda_k1")[0], f("diff_lambda_k2")[0]])[None, :].copy()
    sh["relt"] = f("rel_bias_table").copy()
    import ml_dtypes
    bf = ml_dtypes.bfloat16
    cb = np.zeros((128, 4 * 128), np.float32)
    cb[0:64, 384:512] = 1.0
    cb[:, 0:128] = 1.0
    cb[0:64, 128:192] = 1.0
    cb[64:128, 192:256] = 1.0
    cb[p, 256 + 127 - p] = 1.0
    sh["cbf"] = cb.astype(bf)
    m = np.arange(NR)
    bk = t5_bucket_np(767 - m)
    oh = np.zeros((32, NR + 256), np.float32)
    oh[bk, m] = 1.0
    oh[15, NR:NR + 128] = 1.0
    oh[31, NR + 128:NR + 256] = 1.0
    sh["oneh"] = oh
    w_in = f("attn_w_in")[0]
    ar = np.arange
    sh["w_in_st"] = _st(w_in, np.concatenate([ar(0, 2048), ar(3072, 3840)]), 128)
    _kr = ar(3840, 3904)
    _ks = np.concatenate([ar(3872, 3904), ar(3840, 3872)])
    sh["w_in_kr"] = _st(w_in, np.concatenate([_kr, _kr, _ks, _ks]), 128)
    sh["w_in_v"] = _st(w_in, ar(2048, 3072), 128)
    uq = f("mla_w_uq")[0]
    sh["w_uq_n"] = _st(uq, np.concatenate([ar(192 * h, 192 * h + 128) for h in range(8)]), 128)
    sh["w_uq_r"] = _st(uq, np.concatenate([np.concatenate([ar(192 * h + 128, 192 * h + 192),
                                                            ar(192 * h + 128, 192 * h + 192),
                                                            ar(192 * h + 160, 192 * h + 192),
                                                            ar(192 * h + 128, 192 * h + 160),
                                                            ar(192 * h + 160, 192 * h + 192),
                                                            ar(192 * h + 128, 192 * h + 160)])
                                           for h in range(8)]), 128)
    ukv = f("mla_w_ukv")[0]
    sh["w_ukv_k"] = _st(ukv, np.concatenate([ar(256 * h, 256 * h + 128) for h in range(8)]), 128)
    sh["w_ukv_v"] = _st(ukv, np.concatenate([ar(256 * h + 128, 256 * h + 256) for h in range(8)]), 128)
    sh["w_out"] = _st(f("attn_w_out")[0], ar(2048), 128)
    cwi = f("conv_w_in")[0]
    sh["cw_in"] = _st(cwi, np.concatenate([np.concatenate([ar(128 * j, 128 * j + 128) + 2048 * t
                                                           for t in range(3)]) for j in range(16)]), 128)
    sh["cw_out"] = _st(f("conv_w_out")[0], ar(2048), 128)
    for l in range(2):
        sh["wg%d" % l] = _st(f("ffn_w_gate")[l], ar(DFF), 128)
        sh["wu%d" % l] = _st(f("ffn_w_up")[l], ar(DFF), 128)
        wd = f("ffn_w_down")[l]
        t = wd.reshape(4, GRP, 128, 16, 128)
        sh["wd%d" % l] = np.ascontiguousarray(t.transpose(0, 3, 2, 1, 4)).reshape(64, 128, GRP * 128)
    return sh


SHAPES = dict(vec=[128, NV], lamv=[1, 256], relt=[32, 8], oneh=[32, NR + 256],
              w_in_st=[22, 128, 2048], w_in_kr=[2, 128, 2048], w_in_v=[8, 128, 2048],
              w_uq_n=[8, 128, 512], w_uq_r=[16, 128, 512], w_ukv_k=[8, 128, 256], w_ukv_v=[8, 128, 256],
              w_out=[16, 128, 2048], cw_in=[48, 128, 2048], cw_out=[16, 128, 2048],
              wg0=[NCH, 128, 2048], wu0=[NCH, 128, 2048], wd0=[64, 128, GRP * 128],
              wg1=[NCH, 128, 2048], wu1=[NCH, 128, 2048], wd1=[64, 128, GRP * 128])


class K:
    def __init__(self, phases):
        self.p = Prog()
        p = self.p
        nc = p.nc
        self.nc = nc
        self.phases = phases
        self.A = {}
        for k, shp in SHAPES.items():
            self.A[k] = nc.dram_tensor(k, shp, F32, kind="ExternalInput").ap()
        self.cbf_d = nc.dram_tensor("cbf", [128, 512], BF16, kind="ExternalInput").ap()
        self.pos_d = nc.dram_tensor("pos", [1, S], I32, kind="ExternalInput").ap()
        self.x_in = nc.dram_tensor("xT", [16, 128, S], F32, kind="ExternalInput").ap()
        self.x_out = nc.dram_tensor("yT", [16, 128, S], F32, kind="ExternalOutput").ap()
        self.xs = [nc.dram_tensor("xs%d" % i, [16, 128, S], F32, kind="Internal").ap() for i in range(3)]
        self.PS = p.ps("PS", [128, 8, 512], F32)
        self.bank = [Buf(excl=True) for _ in range(8)]
        self.vec = p.sb("vec", [128, NV], F32)
        self.cbf = p.sb("cbfs", [128, 512], BF16)
        self.cb = Buf()
        p.dma("sp", self.vec[:], self.A["vec"], writes=[self.cb], key=self.cb)
        self.cb2 = Buf()
        p.dma("sp", self.cbf[:], self.cbf_d, writes=[self.cb2], key=self.cb2)
        self.ones = self.cbf[:, 0:128]
        self.blk = self.cbf[:, 128:256]
        self.anti = self.cbf[:, 256:384]
        self.oneslo = self.cbf[:, 384:512]
        self.wq = 0

    def bk(self, i):
        return self.PS[:, i, :]

    def vcol(self, name, i=0, rows=128):
        c = VC[name] + i
        return self.vec[0:rows, c:c + 1]

    def wload(self, slot, sbuf, src, width):
        p = self.p
        if width > 2048:
            o = sbuf[:, 0:width].rearrange("p (a b) -> p a b", b=2048)
            i = src.rearrange("p (a b) -> p a b", b=2048)
        else:
            o = sbuf[:, 0:width]
            i = src
        p.dma("pool", o, i, writes=[slot], key=slot)

    def norm(self, xsrc, xbufs, gname, hT, hb):
        p = self.p
        p.push()
        NX = 6
        xt = [p.sb("nx%d" % i, [128, S], F32) for i in range(NX)]
        xtb = [Buf() for _ in range(NX)]
        sq = [p.sb("nsq%d" % i, [128, S], BF16) for i in range(2)]
        sqb = [Buf() for _ in range(2)]
        rstd = p.sb("nrstd", [128, S], F32)
        rb = Buf()
        for kc in range(min(NX - 1, 16)):
            p.dma("sp", xt[kc % NX][:], xsrc[kc], reads=xbufs[kc], writes=[xtb[kc % NX]], key=xtb[kc % NX])
        for kc in range(16):
            s = kc % NX
            k2 = kc + NX - 1
            if k2 < 16:
                p.dma("sp", xt[k2 % NX][:], xsrc[k2], reads=xbufs[k2], writes=[xtb[k2 % NX]], key=xtb[k2 % NX])
            p.op("act", lambda e: e.activation(out=sq[kc % 2][:], in_=xt[s][:], func=AF.Square),
                 reads=[xtb[s]], writes=[sqb[kc % 2]])
            p.op("dve", lambda e: e.tensor_scalar(out=hT[:, kc, :], in0=xt[s][:], scalar1=self.vcol(gname, kc),
                                                  scalar2=None, op0=ALU.mult),
                 reads=[xtb[s], self.cb], writes=[hb[kc]])
            for j in range(4):
                p.op("pe", lambda e: e.matmul(self.bk(j), self.ones, sq[kc % 2][:, 512 * j:512 * j + 512],
                                              start=(kc == 0), stop=(kc == 15)),
                     reads=[sqb[kc % 2], self.cb2], writes=[self.bank[j]], inc=(j == 3))
        for j in range(4):
            p.op("act", lambda e: e.activation(out=rstd[:, 512 * j:512 * j + 512], in_=self.bk(j),
                                               func=AF.Sqrt, scale=1.0 / D, bias=self.eps_ap()),
                 reads=[self.bank[j]], writes=[rb])
        p.op("dve", lambda e: e.reciprocal(out=rstd[:], in_=rstd[:]), reads=[rb], writes=[rb])
        for kc in range(16):
            eng = "dve"
            p.op(eng, lambda e: e.tensor_tensor(out=hT[:, kc, :], in0=hT[:, kc, :], in1=rstd[:], op=ALU.mult),
                 reads=[rb], writes=[hb[kc]])
        p.pop()

    def eps_ap(self):
        if not hasattr(self, "_eps"):
            p = self.p
            st = p.scope
            p.scope = None
            self._eps = p.sb("epsc", [128, 1], F32)
            p.scope = st
            self._epsb = Buf()
            p.op("dve", lambda e: e.memset(self._eps[:], EPS), writes=[self._epsb])
            p.barrier()
        return self._eps[:, 0:1]

    def ffn(self, l, xsrc, xsb, xdst, xdb):
        p = self.p
        p.push()
        hT = p.sb("hT", [128, 16, S], BF16)
        hb = [Buf() for _ in range(16)]
        self.norm(xsrc, xsb, "g_ffn%d" % l, hT, hb)
        act = p.sb("fact", [128, GRP, S], BF16)
        ab = [Buf() for _ in range(GRP)]
        NSL = 3
        wg = [p.sb("fwg%d" % i, [128, 2048], BF16) for i in range(NSL)]
        wu = [p.sb("fwu%d" % i, [128, 2048], BF16) for i in range(NSL)]
        wgb = [Buf() for _ in range(NSL)]
        wub = [Buf() for _ in range(NSL)]
        wd = [p.sb("fwd%d" % i, [128, GRP * 128], BF16) for i in range(NSL)]
        wdb = [Buf() for _ in range(NSL)]
        y = [p.sb("fy%d" % i, [128, S], F32) for i in range(2)]
        yb = [Buf() for _ in range(2)]
        sg = [p.sb("fs%d" % i, [128, S], BF16) for i in range(2)]
        sgb = [Buf() for _ in range(2)]
        NXO, PF = 6, 4
        xo = [p.sb("fxo%d" % i, [128, 512], F32) for i in range(NXO)]
        xob = [Buf() for _ in range(NXO)]
        xn = [p.sb("fxn%d" % i, [128, 512], F32) for i in range(4)]
        xnb = [Buf() for _ in range(4)]
        Wg, Wu, Wd = self.A["wg%d" % l], self.A["wu%d" % l], self.A["wd%d" % l]
        fw, fb = "fw%d" % l, "fb%d" % l
        ev = 0
        self.wload(wgb[0], wg[0], Wg[0], 2048)
        self.wload(wub[0], wu[0], Wu[0], 2048)
        for grp in range(4):
            for cl in range(GRP):
                c = grp * GRP + cl
                s = c % NSL
                if c + 1 < NCH:
                    s1 = (c + 1) % NSL
                    self.wload(wgb[s1], wg[s1], Wg[c + 1], 2048)
                    self.wload(wub[s1], wu[s1], Wu[c + 1], 2048)
                for half, (wt, wtb) in enumerate(((wg[s], wgb[s]), (wu[s], wub[s]))):
                    for j in range(4):
                        bi = half * 4 + j
                        for kc in range(16):
                            p.op("pe", lambda e: e.matmul(self.bk(bi), wt[:, kc * 128:(kc + 1) * 128],
                                                          hT[:, kc, 512 * j:512 * j + 512],
                                                          start=(kc == 0), stop=(kc == 15)),
                                 reads=[wtb, hb[kc]], writes=[self.bank[bi]], inc=(kc == 15))
                ys = c % 2
                yt, ytb = y[ys], yb[ys]
                w0, w1, w2 = (self.vcol(fw, 3 * c + t) for t in range(3))
                bcol = self.vcol(fb, c)
                for j in range(4):
                    p.op("dve", lambda e: e.tensor_scalar(out=yt[:, 512 * j:512 * j + 512], in0=self.bk(j),
                                                          scalar1=w1, scalar2=bcol, op0=ALU.mult, op1=ALU.add),
                         reads=[self.bank[j], self.cb], writes=[ytb])
                for j in range(4):
                    n = 512 if j < 3 else 511
                    p.op("dve", lambda e: e.scalar_tensor_tensor(
                        out=yt[:, 512 * j + 1:512 * j + 1 + n], in0=self.PS[:, j, 0:n], scalar=w0,
                        in1=yt[:, 512 * j + 1:512 * j + 1 + n], op0=ALU.mult, op1=ALU.add),
                         reads=[self.bank[j], ytb], writes=[ytb])
                for j in range(4):
                    lo = 0 if j > 0 else 1
                    n = 512 - lo
                    p.op("dve", lambda e: e.scalar_tensor_tensor(
                        out=yt[:, 512 * j + lo - 1:512 * j + lo - 1 + n], in0=self.PS[:, j, lo:512], scalar=w2,
                        in1=yt[:, 512 * j + lo - 1:512 * j + lo - 1 + n], op0=ALU.mult, op1=ALU.add),
                         reads=[self.bank[j], ytb], writes=[ytb])
                st, stb = sg[ys], sgb[ys]
                p.op("act", lambda e: e.activation(out=st[:], in_=yt[:], func=AF.Silu),
                     reads=[ytb], writes=[stb])
                for j in range(4):
                    p.op("dve", lambda e: e.tensor_tensor(out=act[:, cl, 512 * j:512 * j + 512],
                                                          in0=self.bk(4 + j), in1=st[:, 512 * j:512 * j + 512],
                                                          op=ALU.mult),
                         reads=[self.bank[4 + j], stb], writes=[ab[cl]])
            src = xsrc if grp == 0 else xdst
            srcb = xsb if grp == 0 else xdb
            tiles = [(dc, j) for dc in range(16) for j in range(4)]

            def ld(t):
                dc_, j_ = tiles[t]
                es_ = (ev0 + t) % NXO
                p.dma("sp", xo[es_][:], src[dc_][:, 512 * j_:512 * j_ + 512], reads=[srcb[dc_][j_]],
                      writes=[xob[es_]], key=xob[es_])
            ev0 = ev
            for t in range(PF):
                ld(t)
            self.wload(wdb[0], wd[0], Wd[grp * 16 + 0], GRP * 128)
            for t, (dc, j) in enumerate(tiles):
                s = dc % NSL
                if j == 0 and dc + 1 < 16:
                    s1 = (dc + 1) % NSL
                    self.wload(wdb[s1], wd[s1], Wd[grp * 16 + dc + 1], GRP * 128)
                if t + PF < len(tiles):
                    ld(t + PF)
                bi = ev % 8
                es = ev % NXO
                en = ev % 4
                ev += 1
                for cl in range(GRP):
                    p.op("pe", lambda e: e.matmul(self.bk(bi), wd[s][:, cl * 128:(cl + 1) * 128],
                                                  act[:, cl, 512 * j:512 * j + 512],
                                                  start=(cl == 0), stop=(cl == GRP - 1)),
                         reads=[wdb[s], ab[cl]], writes=[self.bank[bi]], inc=(cl == GRP - 1))
                p.op("dve", lambda e: e.tensor_tensor(out=xn[en][:], in0=self.bk(bi), in1=xo[es][:], op=ALU.add),
                     reads=[self.bank[bi], xob[es]], writes=[xnb[en]])
                p.dma("sp", xdst[dc][:, 512 * j:512 * j + 512], xn[en][:], reads=[xnb[en]],
                      writes=[xdb[dc][j]], key=xnb[en])
        p.pop()

    def outproj(self, Wd, actT, actb, xsrc, xsb, xdst, xdb):
        p = self.p
        NSL = 3
        NXO, PF = 6, 4
        w = [p.sb("ow%d" % i, [128, 2048], BF16) for i in range(NSL)]
        wb = [Buf() for _ in range(NSL)]
        xo = [p.sb("oxo%d" % i, [128, 512], F32) for i in range(NXO)]
        xob = [Buf() for _ in range(NXO)]
        xn = [p.sb("oxn%d" % i, [128, 512], F32) for i in range(4)]
        xnb = [Buf() for _ in range(4)]
        tiles = [(dc, j) for dc in range(16) for j in range(4)]

        def ld(t):
            dc_, j_ = tiles[t]
            p.dma("sp", xo[t % NXO][:], xsrc[dc_][:, 512 * j_:512 * j_ + 512], reads=[xsb[dc_][j_]],
                  writes=[xob[t % NXO]], key=xob[t % NXO])
        for t in range(PF):
            ld(t)
        self.wload(wb[0], w[0], Wd[0], 2048)
        for t, (dc, j) in enumerate(tiles):
            s = dc % NSL
            if j == 0 and dc + 1 < 16:
                self.wload(wb[(dc + 1) % NSL], w[(dc + 1) % NSL], Wd[dc + 1], 2048)
            if t + PF < len(tiles):
                ld(t + PF)
            bi = t % 8
            es = t % NXO
            en = t % 4
            for kc in range(16):
                p.op("pe", lambda e: e.matmul(self.bk(bi), w[s][:, kc * 128:(kc + 1) * 128],
                                              actT[:, kc, 512 * j:512 * j + 512],
                                              start=(kc == 0), stop=(kc == 15)),
                     reads=[wb[s], actb[kc]], writes=[self.bank[bi]], inc=(kc == 15))
            p.op("dve", lambda e: e.tensor_tensor(out=xn[en][:], in0=self.bk(bi), in1=xo[es][:], op=ALU.add),
                 reads=[self.bank[bi], xob[es]], writes=[xnb[en]])
            p.dma("sp", xdst[dc][:, 512 * j:512 * j + 512], xn[en][:], reads=[xnb[en]],
                  writes=[xdb[dc][j]], key=xnb[en])

    def conv(self, xsrc, xsb, xdst, xdb):
        p = self.p
        p.push()
        hT = p.sb("chT", [128, 16, S], BF16)
        hb = [Buf() for _ in range(16)]
        self.norm(xsrc, xsb, "g_conv", hT, hb)
        mT = p.sb("cmT", [128, 16, S], BF16)
        mb = [Buf() for _ in range(16)]
        p.push()
        w = [p.sb("cw%d" % i, [128, 2048], BF16) for i in range(6)]
        wb = [Buf() for _ in range(6)]
        cgs = p.sb("ccg", [128, S], F32)
        cgb = Buf()
        z = p.sb("cz", [128, S], F32)
        zb = Buf()
        cv = p.sb("ccv", [128, S], F32)
        cvb = Buf()
        W = self.A["cw_in"]
        for t in range(3):
            self.wload(wb[t], w[t], W[t], 2048)
        for j in range(16):
            o = (j % 2) * 3
            if j + 1 < 16:
                o1 = ((j + 1) % 2) * 3
                for t in range(3):
                    self.wload(wb[o1 + t], w[o1 + t], W[3 * (j + 1) + t], 2048)

            def proj(t, b0):
                for jj in range(4):
                    for kc in range(16):
                        p.op("pe", lambda e: e.matmul(self.bk(b0 + jj), w[o + t][:, kc * 128:(kc + 1) * 128],
                                                      hT[:, kc, 512 * jj:512 * jj + 512],
                                                      start=(kc == 0), stop=(kc == 15)),
                             reads=[wb[o + t], hb[kc]], writes=[self.bank[b0 + jj]], inc=(kc == 15))
            proj(1, 0)
            proj(2, 4)
            for jj in range(4):
                p.op("act", lambda e: e.activation(out=cgs[:, 512 * jj:512 * jj + 512], in_=self.bk(jj), func=AF.Copy),
                     reads=[self.bank[jj]], writes=[cgb])
            proj(0, 0)
            for jj in range(4):
                p.op("dve", lambda e: e.tensor_tensor(out=z[:, 512 * jj:512 * jj + 512], in0=self.bk(4 + jj),
                                                      in1=cgs[:, 512 * jj:512 * jj + 512], op=ALU.mult),
                     reads=[self.bank[4 + jj], cgb], writes=[zb])
            w0, w1, w2 = (self.vcol("cw", 3 * j + t) for t in range(3))
            p.op("dve", lambda e: e.tensor_scalar(out=cv[:], in0=z[:], scalar1=w1, scalar2=None, op0=ALU.mult),
                 reads=[zb, self.cb], writes=[cvb])
            p.op("dve", lambda e: e.scalar_tensor_tensor(out=cv[:, 1:S], in0=z[:, 0:S - 1], scalar=w0,
                                                         in1=cv[:, 1:S], op0=ALU.mult, op1=ALU.add),
                 reads=[zb, cvb], writes=[cvb])
            p.op("dve", lambda e: e.scalar_tensor_tensor(out=cv[:, 0:S - 1], in0=z[:, 1:S], scalar=w2,
                                                         in1=cv[:, 0:S - 1], op0=ALU.mult, op1=ALU.add),
                 reads=[zb, cvb], writes=[cvb])
            for jj in range(4):
                p.op("dve", lambda e: e.tensor_tensor(out=mT[:, j, 512 * jj:512 * jj + 512], in0=self.bk(jj),
                                                      in1=cv[:, 512 * jj:512 * jj + 512], op=ALU.mult),
                     reads=[self.bank[jj], cvb], writes=[mb[j]])
        p.pop()
        p.push()
        self.outproj(self.A["cw_out"], mT, mb, xsrc, xsb, xdst, xdb)
        p.pop()
        p.pop()

    def attn(self, xsrc, xsb, xdst, xdb):
        p = self.p
        nc = self.nc
        AX = mybir.AxisListType.X

        def dsc(name, shape, dt=BF16):
            return nc.dram_tensor(name, shape, dt, kind="Internal").ap()
        AQK = dsc("AQK", [24, 128, S]); bAQK = [Buf() for _ in range(24)]
        VA = dsc("VA", [8, 128, S]); bVA = [Buf() for _ in range(8)]
        QN = dsc("QN", [8, 128, S]); bQN = [Buf() for _ in range(8)]
        KN = dsc("KN", [8, 128, S]); bKN = [Buf() for _ in range(8)]
        QR = dsc("QR", [8, 128, S]); bQR = [Buf() for _ in range(8)]
        KR = dsc("KR", [8, 128, S]); bKR = [Buf() for _ in range(8)]
        VB = dsc("VB", [8, 128, S]); bVB = [Buf() for _ in range(8)]
        OT = dsc("OT", [16, 128, S]); bOT = [Buf() for _ in range(16)]
        TR = dsc("TR", [8, NR], F32); bTR = Buf()
        SCA = 64 ** -0.5
        SCB = 192 ** -0.5
        p.push()
        cs = p.sb("acs", [128, 16], F32); csb = Buf()
        far = p.sb("afar", [128, 16], F32); farb = Buf()
        p.push()
        lamt = p.sb("lamt", [128, 256], F32); lb = Buf()
        p.dma("sp", lamt[:], self.A["lamv"][0].partition_broadcast(128), writes=[lb], key=lb)
        prod = p.sb("lprod", [128, 128], F32); pb = Buf()
        red = p.sb("lred", [128, 4], F32); rb_ = Buf()
        p.op("dve", lambda e: e.tensor_tensor(out=prod[:], in0=lamt[:, 0:128], in1=lamt[:, 128:256], op=ALU.mult),
             reads=[lb], writes=[pb])
        p.op("dve", lambda e: e.tensor_reduce(out=red[:, 0:2], in_=prod[:].rearrange("p (a b) -> p a b", b=64),
                                              axis=AX, op=ALU.add), reads=[pb], writes=[rb_])
        p.op("act", lambda e: e.activation(out=red[:, 2:4], in_=red[:, 0:2], func=AF.Exp), reads=[rb_], writes=[rb_])
        p.op("dve", lambda e: e.tensor_tensor(out=cs[:, 2:3], in0=red[:, 3:4], in1=red[:, 2:3], op=ALU.subtract),
             reads=[rb_], writes=[csb])
        p.op("dve", lambda e: e.tensor_scalar(out=cs[:, 2:3], in0=cs[:, 2:3], scalar1=-LAM_INIT0, scalar2=None,
                                              op0=ALU.add), reads=[csb], writes=[csb])
        for col, nm, sc in ((0, "dq", SCA), (1, "subln", 1.0 - LAM_INIT0), (3, "mq_n", SCB), (4, "mq_r", SCB),
                            (5, "mq_s", SCB)):
            p.op("dve", lambda e: e.tensor_scalar(out=cs[:, col:col + 1], in0=self.vcol(nm), scalar1=float(sc),
                                                  scalar2=None, op0=ALU.mult), reads=[self.cb, csb], writes=[csb])
        relt = p.sb("relt", [32, 8], BF16); rlb = Buf()
        oneh = p.sb("oneh", [32, NR + 256], BF16); ohb = Buf()
        p.dma("pool", relt[:], self.A["relt"], writes=[rlb], key=rlb)
        p.dma("pool", oneh[:], self.A["oneh"], writes=[ohb], key=ohb)
        trs = p.sb("trs", [8, NR], F32); trb = Buf()
        for c in range(3):
            p.op("pe", lambda e: e.matmul(self.PS[0:8, c, :], relt[:, :], oneh[:, 512 * c:512 * c + 512],
                                          start=True, stop=True), reads=[rlb, ohb], writes=[self.bank[c]])
            p.op("act", lambda e: e.activation(out=trs[:, 512 * c:512 * c + 512], in_=self.PS[0:8, c, :], func=AF.Copy),
                 reads=[self.bank[c]], writes=[trb])
        p.dma("sp", TR, trs[:], reads=[trb], writes=[bTR], key=trb)
        for c in range(2):
            p.op("pe", lambda e: e.matmul(self.PS[:, 3 + c, 0:8], oneh[:, NR + 128 * c:NR + 128 * c + 128], relt[:, :],
                                          start=True, stop=True), reads=[rlb, ohb], writes=[self.bank[3 + c]])
            p.op("act", lambda e: e.activation(out=far[:, 8 * c:8 * c + 8], in_=self.PS[:, 3 + c, 0:8], func=AF.Copy),
                 reads=[self.bank[3 + c]], writes=[farb])
        p.pop()
        STG = int(os.environ.get("ATT_STAGE", "9"))
        if STG < 1:
            p.pop()
            return
        p.push()
        hT = p.sb("ahT", [128, 16, S], BF16)
        hb = [Buf() for _ in range(16)]
        self.norm(xsrc, xsb, "g_attn", hT, hb)
        Ct = p.sb("ropeC", [128, S], F32); Ctb = Buf()
        St = p.sb("ropeS", [128, S], F32); Stb = Buf()
        p.push()
        posi = p.sb("posi", [128, S], I32); pib = Buf()
        p.dma("sp", posi[:], self.pos_d[0].partition_broadcast(128), writes=[pib], key=pib)
        posf = p.sb("posf", [128, S], F32); pfb = Buf()
        p.op("dve", lambda e: e.tensor_copy(out=posf[:], in_=posi[:]), reads=[pib], writes=[pfb])
        tt_ = p.sb("rt", [128, S], F32); tb_ = Buf()
        ti = p.sb("rti", [128, S], I32); tib = Buf()
        tf = p.sb("rtf", [128, S], F32); tfb = Buf()
        for tab, tabb, addq in ((St, Stb, 0.0), (Ct, Ctb, 0.25)):
            p.op("dve", lambda e: e.tensor_scalar(out=tt_[:], in0=posf[:], scalar1=self.vcol("invf"),
                                                  scalar2=float(addq), op0=ALU.mult, op1=ALU.add),
                 reads=[pfb, self.cb], writes=[tb_])
            p.op("dve", lambda e: e.tensor_copy(out=ti[:], in_=tt_[:]), reads=[tb_], writes=[tib])
            p.op("dve", lambda e: e.tensor_copy(out=tf[:], in_=ti[:]), reads=[tib], writes=[tfb])
            p.op("dve", lambda e: e.tensor_tensor(out=tt_[:], in0=tt_[:], in1=tf[:], op=ALU.subtract),
                 reads=[tfb, tb_], writes=[tb_])
            p.op("dve", lambda e: e.tensor_scalar(out=tf[:], in0=tt_[:], scalar1=0.5, scalar2=None, op0=ALU.is_gt),
                 reads=[tb_], writes=[tfb])
            p.op("dve", lambda e: e.tensor_tensor(out=tt_[:], in0=tt_[:], in1=tf[:], op=ALU.subtract),
                 reads=[tfb, tb_], writes=[tb_])
            p.op("dve", lambda e: e.tensor_scalar(out=tf[:], in0=tt_[:], scalar1=-0.5, scalar2=None, op0=ALU.is_lt),
                 reads=[tb_], writes=[tfb])
            p.op("dve", lambda e: e.tensor_tensor(out=tt_[:], in0=tt_[:], in1=tf[:], op=ALU.add),
                 reads=[tfb, tb_], writes=[tb_])
            p.op("act", lambda e: e.activation(out=tab[:], in_=tt_[:], func=AF.Sin,
                                               scale=float(2 * math.pi * (1 - 1e-6))),
                 reads=[tb_], writes=[tabb])
        p.op("dve", lambda e: e.tensor_scalar(out=St[:], in0=St[:], scalar1=self.vcol("sgn"), scalar2=None,
                                              op0=ALU.mult), reads=[Stb, self.cb], writes=[Stb])
        p.pop()
        if STG < 2:
            p.pop()
            p.pop()
            return
        cqn = p.sb("cqn", [128, 4, S], BF16); cqb = [Buf() for _ in range(4)]
        ckn = p.sb("ckn", [128, 2, S], BF16); ckb = [Buf() for _ in range(2)]
        sqkr = p.sb("sqkr", [128, S], BF16); sqkb = Buf()
        krbase = p.sb("krbase", [128, S], F32); krb = Buf()
        NW = 3
        ws = [p.sb("aw%d" % i, [128, 2048], BF16) for i in range(NW)]
        wsb = [Buf() for _ in range(NW)]
        osl = [p.sb("aos%d" % i, [128, S], BF16) for i in range(3)]
        oslb = [Buf() for _ in range(3)]
        orl = [p.sb("aor%d" % i, [128, S], BF16) for i in range(2)]
        orlb = [Buf() for _ in range(2)]
        sq = [p.sb("asq%d" % i, [128, 512], BF16) for i in range(2)]
        sqb = [Buf() for _ in range(2)]
        sq2 = [p.sb("asr%d" % i, [128, 512], BF16) for i in range(2)]
        sq2b = [Buf() for _ in range(2)]
        rr = [p.sb("arr%d" % i, [128, 512], F32) for i in range(2)]
        rrb = [Buf() for _ in range(2)]
        raw = p.sb("araw", [128, 4, 512], F32); rawb = [Buf() for _ in range(4)]
        t1 = p.sb("at1", [128, 512], F32); t1b = Buf()
        t2 = p.sb("at2", [128, 512], F32); t2b = Buf()
        st = {"w": 0, "o": 0, "t": 0, "bk": 0}

        def nextw(src, width):
            i = st["w"] % NW
            st["w"] += 1
            self.wload(wsb[i], ws[i], src, width)
            return ws[i], wsb[i]

        def nbank():
            b = st["bk"] % 6
            st["bk"] += 1
            return b

        def proj(w, wb_, KC, M, src, srcb, j, b, rows=None):
            for kc in range(KC):
                p.op("pe", lambda e: e.matmul(self.PS[0:M, b, :], w[:, kc * M:(kc + 1) * M],
                                              src[:, kc, 512 * j:512 * j + 512], start=(kc == 0), stop=(kc == KC - 1)),
                     reads=[wb_, srcb[kc]], writes=[self.bank[b]], inc=(kc == KC - 1))

        def rstd_from(bB, nfeat, k):
            p.op("act", lambda e: e.activation(out=rr[k][:], in_=self.bk(bB), func=AF.Sqrt, scale=1.0 / nfeat,
                                               bias=self.eps_ap()), reads=[self.bank[bB]], writes=[rrb[k]])
            p.op("dve", lambda e: e.reciprocal(out=rr[k][:], in_=rr[k][:]), reads=[rrb[k]], writes=[rrb[k]])

        W = self.A
        SUB = int(os.environ.get("ATT_SUB", "9"))

        def bail():
            p.pop()
            p.pop()
        for c in range(16):
            w, wb_ = nextw(W["w_in_st"][c], 2048)
            o = 0
            gcol = cs[:, 0:1] if c < 8 else self.vcol("dk_lo")
            for j in range(4):
                k = st["t"] % 2
                st["t"] += 1
                bA, bB = nbank(), 6 + k
                proj(w, wb_, 16, 128, hT, hb, j, bA)
                p.op("act", lambda e: e.activation(out=sq[k][:], in_=self.bk(bA), func=AF.Square),
                     reads=[self.bank[bA]], writes=[sqb[k]])
                p.op("pe", lambda e: e.matmul(self.bk(bB), self.blk, sq[k][:], start=True, stop=True),
                     reads=[sqb[k], self.cb2], writes=[self.bank[bB]])
                rstd_from(bB, 64, k)
                p.op("dve", lambda e: e.scalar_tensor_tensor(out=osl[o][:, 512 * j:512 * j + 512], in0=self.bk(bA),
                                                             scalar=gcol, in1=rr[k][:], op0=ALU.mult, op1=ALU.mult),
                     reads=[self.bank[bA], rrb[k], csb, self.cb], writes=[oslb[o]])
                if c >= 8:
                    p.op("dve", lambda e: e.scalar_tensor_tensor(out=osl[2][:, 512 * j:512 * j + 512], in0=self.bk(bA),
                                                                 scalar=self.vcol("dk_hi"), in1=rr[k][:], op0=ALU.mult,
                                                                 op1=ALU.mult),
                         reads=[self.bank[bA], rrb[k], self.cb], writes=[oslb[2]])
            p.dma("sp", AQK[c], osl[o][:], reads=[oslb[o]], writes=[bAQK[c]], key=oslb[o])
            if c >= 8:
                p.dma("sp", AQK[c + 8], osl[2][:], reads=[oslb[2]], writes=[bAQK[c + 8]], key=oslb[2])

        def vproj(wsrc, KC, src, srcb, dst, dstb):
            for h in range(8):
                w, wb_ = nextw(wsrc[h], KC * 128)
                o = st["o"] % 2
                st["o"] += 1
                for q4 in range(4):
                    b = nbank()
                    for t4 in range(4):
                        tt = q4 * 4 + t4
                        for kc in range(KC):
                            p.op("pe", lambda e: e.matmul(self.PS[:, b, 128 * t4:128 * t4 + 128],
                                                          src[:, kc, 128 * tt:128 * tt + 128],
                                                          w[:, kc * 128:(kc + 1) * 128],
                                                          start=(kc == 0), stop=(kc == KC - 1)),
                                 reads=[wb_, srcb[kc]], writes=[self.bank[b]], inc=(kc == KC - 1 and t4 == 3))
                    p.op("act", lambda e: e.activation(out=osl[o][:, 512 * q4:512 * q4 + 512], in_=self.bk(b),
                                                       func=AF.Copy), reads=[self.bank[b]], writes=[oslb[o]])
                p.dma("sp", dst[h], osl[o][:], reads=[oslb[o]], writes=[dstb[h]], key=oslb[o])
        if SUB < 2:
            return bail()
        vproj(W["w_in_v"], 16, hT, hb, VA, bVA)
        if SUB < 3:
            return bail()

        for base, nchk, dstT, dstb_, gname in ((16, 4, cqn, cqb, "qa"), (20, 2, ckn, ckb, "kva")):
            wl = [nextw(W["w_in_st"][base + c], 2048) for c in range(nchk)] if nchk <= NW else None
            for j in range(4):
                k = st["t"] % 2
                st["t"] += 1
                bB = 6 + k
                for c in range(nchk):
                    if wl is None:
                        w, wb_ = nextw(W["w_in_st"][base + c], 2048)
                    else:
                        w, wb_ = wl[c]
                    bA = nbank()
                    proj(w, wb_, 16, 128, hT, hb, j, bA)
                    p.op("act", lambda e: e.activation(out=raw[:, c, :], in_=self.bk(bA), func=AF.Copy),
                         reads=[self.bank[bA]], writes=[rawb[c]])
                    p.op("act", lambda e: e.activation(out=sq[c % 2][:], in_=self.bk(bA), func=AF.Square),
                         reads=[self.bank[bA]], writes=[sqb[c % 2]])
                    p.op("pe", lambda e: e.matmul(self.bk(bB), self.ones, sq[c % 2][:], start=(c == 0),
                                                  stop=(c == nchk - 1)),
                         reads=[sqb[c % 2], self.cb2], writes=[self.bank[bB]])
                rstd_from(bB, 128 * nchk, k)
                for c in range(nchk):
                    p.op("dve", lambda e: e.scalar_tensor_tensor(out=dstT[:, c, 512 * j:512 * j + 512], in0=raw[:, c, :],
                                                                 scalar=self.vcol(gname, c), in1=rr[k][:],
                                                                 op0=ALU.mult, op1=ALU.mult),
                         reads=[rawb[c], rrb[k], self.cb], writes=[dstb_[c]])

        def rope_mix(bR, bS, gr, gs, j, outap, outb, extra_reads=()):
            p.op("dve", lambda e: e.scalar_tensor_tensor(out=t1[:], in0=self.bk(bR), scalar=gr,
                                                         in1=Ct[:, 512 * j:512 * j + 512], op0=ALU.mult, op1=ALU.mult),
                 reads=[self.bank[bR], Ctb, csb, self.cb], writes=[t1b])
            p.op("dve", lambda e: e.scalar_tensor_tensor(out=t2[:], in0=self.bk(bS), scalar=gs,
                                                         in1=St[:, 512 * j:512 * j + 512], op0=ALU.mult, op1=ALU.mult),
                 reads=[self.bank[bS], Stb, csb, self.cb], writes=[t2b])
            p.op("dve", lambda e: e.tensor_tensor(out=outap, in0=t1[:], in1=t2[:], op=ALU.add),
                 reads=[t1b, t2b] + list(extra_reads), writes=[outb])

        if SUB < 4:
            return bail()
        wkr, wkrb = nextw(W["w_in_kr"][0], 2048)
        wks, wksb = nextw(W["w_in_kr"][1], 2048)
        for j in range(4):
            bA, bS = nbank(), nbank()
            proj(wkr, wkrb, 16, 128, hT, hb, j, bA)
            proj(wks, wksb, 16, 128, hT, hb, j, bS)
            p.op("act", lambda e: e.activation(out=sqkr[:, 512 * j:512 * j + 512], in_=self.bk(bA),
                                               func=AF.Square), reads=[self.bank[bA]], writes=[sqkb])
            if os.environ.get("NOROPE") != "1":
                rope_mix(bA, bS, self.vcol("mk_r"), self.vcol("mk_s"), j,
                         krbase[:, 512 * j:512 * j + 512], krb)

        if SUB < 5:
            return bail()
        for h in range(8):
            wn, wnb = nextw(W["w_uq_n"][h], 512)
            wr, wrb = nextw(W["w_uq_r"][2 * h], 512)
            wsw, wswb = nextw(W["w_uq_r"][2 * h + 1], 512)
            o = st["o"] % 2
            st["o"] += 1
            for j in range(4):
                k = st["t"] % 2
                st["t"] += 1
                bB = 6 + k
                bA, bR, bS = nbank(), nbank(), nbank()
                proj(wn, wnb, 4, 128, cqn, cqb, j, bA)
                proj(wr, wrb, 4, 128, cqn, cqb, j, bR)
                proj(wsw, wswb, 4, 128, cqn, cqb, j, bS)
                p.op("act", lambda e: e.activation(out=sq[k][:], in_=self.bk(bA), func=AF.Square),
                     reads=[self.bank[bA]], writes=[sqb[k]])
                p.op("act", lambda e: e.activation(out=sq2[k][:], in_=self.bk(bR), func=AF.Square),
                     reads=[self.bank[bR]], writes=[sq2b[k]])
                p.op("pe", lambda e: e.matmul(self.bk(bB), self.ones, sq[k][:], start=True, stop=False),
                     reads=[sqb[k], self.cb2], writes=[self.bank[bB]], inc=False)
                p.op("pe", lambda e: e.matmul(self.bk(bB), self.oneslo, sq2[k][:], start=False, stop=True),
                     reads=[sq2b[k], self.cb2], writes=[self.bank[bB]])
                rstd_from(bB, 192, k)
                p.op("dve", lambda e: e.scalar_tensor_tensor(out=osl[o][:, 512 * j:512 * j + 512], in0=self.bk(bA),
                                                             scalar=cs[:, 3:4], in1=rr[k][:], op0=ALU.mult,
                                                             op1=ALU.mult),
                     reads=[self.bank[bA], rrb[k], csb], writes=[oslb[o]])
                rope_mix(bR, bS, cs[:, 4:5], cs[:, 5:6], j, t1[:], t1b)
                p.op("dve", lambda e: e.tensor_tensor(out=orl[o][:, 512 * j:512 * j + 512], in0=t1[:],
                                                      in1=rr[k][:], op=ALU.mult),
                     reads=[t1b, rrb[k]], writes=[orlb[o]])
            p.dma("sp", QN[h], osl[o][:], reads=[oslb[o]], writes=[bQN[h]], key=oslb[o])
            p.dma("sp", QR[h], orl[o][:], reads=[orlb[o]], writes=[bQR[h]], key=orlb[o])
        if SUB < 6:
            return bail()
        for h in range(8):
            wn, wnb = nextw(W["w_ukv_k"][h], 256)
            o = st["o"] % 2
            st["o"] += 1
            for j in range(4):
                k = st["t"] % 2
                st["t"] += 1
                bB = 6 + k
                bA = nbank()
                proj(wn, wnb, 2, 128, ckn, ckb, j, bA)
                p.op("act", lambda e: e.activation(out=sq[k][:], in_=self.bk(bA), func=AF.Square),
                     reads=[self.bank[bA]], writes=[sqb[k]])
                p.op("pe", lambda e: e.matmul(self.bk(bB), self.ones, sq[k][:], start=True, stop=False),
                     reads=[sqb[k], self.cb2], writes=[self.bank[bB]], inc=False)
                p.op("pe", lambda e: e.matmul(self.bk(bB), self.oneslo, sqkr[:, 512 * j:512 * j + 512],
                                              start=False, stop=True),
                     reads=[sqkb, self.cb2], writes=[self.bank[bB]])
                rstd_from(bB, 192, k)
                p.op("dve", lambda e: e.scalar_tensor_tensor(out=osl[o][:, 512 * j:512 * j + 512], in0=self.bk(bA),
                                                             scalar=self.vcol("mk_n"), in1=rr[k][:], op0=ALU.mult,
                                                             op1=ALU.mult),
                     reads=[self.bank[bA], rrb[k], self.cb], writes=[oslb[o]])
                p.op("dve", lambda e: e.tensor_tensor(out=orl[o][:, 512 * j:512 * j + 512],
                                                      in0=krbase[:, 512 * j:512 * j + 512], in1=rr[k][:],
                                                      op=ALU.mult),
                     reads=[krb, rrb[k]], writes=[orlb[o]])
            p.dma("sp", KN[h], osl[o][:], reads=[oslb[o]], writes=[bKN[h]], key=oslb[o])
            p.dma("sp", KR[h], orl[o][:], reads=[orlb[o]], writes=[bKR[h]], key=orlb[o])
        vproj(W["w_ukv_v"], 2, ckn, ckb, VB, bVB)
        p.pop()
        if STG < 3:
            p.pop()
            return

        p.push()
        NH = 2
        qs = [p.sb("uq%d" % i, [128, S], BF16) for i in range(NH)]; qsb = [Buf() for _ in range(NH)]
        ks = [p.sb("uk%d" % i, [128, S], BF16) for i in range(NH)]; ksb = [Buf() for _ in range(NH)]
        vs = [p.sb("uv%d" % i, [128, 16, 128], BF16) for i in range(NH)]; vsb = [Buf() for _ in range(NH)]
        qrs = [p.sb("uqr%d" % i, [128, S], BF16) for i in range(NH)]; qrsb = [Buf() for _ in range(NH)]
        krs = [p.sb("ukr%d" % i, [128, S], BF16) for i in range(NH)]; krsb = [Buf() for _ in range(NH)]
        gb = [p.sb("ug%d" % i, [128, 6, 512], BF16) for i in range(NH)]; gbb = [Buf() for _ in range(NH)]
        NP = 4
        pT = [p.sb("upT%d" % i, [128, 512], BF16) for i in range(NP)]; pTb = [Buf() for _ in range(NP)]
        on = [p.sb("uon%d" % i, [128, S], F32) for i in range(2)]; onb = [Buf() for _ in range(2)]
        rs = [p.sb("urs%d" % i, [128, 512], F32) for i in range(2)]; rsb = [Buf() for _ in range(2)]
        acc = [p.sb("uacc%d" % i, [128, 512], F32) for i in range(2)]; accb = [Buf() for _ in range(2)]
        acb = [p.sb("uacb%d" % i, [128, 512], BF16) for i in range(2)]; acbb = [Buf() for _ in range(2)]
        dd = p.sb("udd", [128, S], F32); ddb = Buf()
        dsq = [p.sb("udsq%d" % i, [128, 512], BF16) for i in range(2)]; dsqb = [Buf() for _ in range(2)]
        drr = [p.sb("udr%d" % i, [128, 512], F32) for i in range(2)]; drrb = [Buf() for _ in range(2)]
        oo = [p.sb("uoo%d" % i, [128, S], BF16) for i in range(2)]; oob = [Buf() for _ in range(2)]
        us = {"s": 0, "p": 0, "q": 0, "o": 0, "d": 0}

        def unit(kind, h, half, sl, oni):
            r0, nr = 0, 128
            kk, kkb = (krs, krsb) if (kind == "a" and half == 1) else (ks, ksb)
            for qt in range(4):
                qi = us["q"] % 2
                us["q"] += 1
                bO, bS = 4 + qi, 6 + qi

                def smm(kt):
                    b = us["s"] % 4
                    us["s"] += 1
                    dl = kt - 4 * qt
                    near = (kind == "a") and (-1 <= dl <= 4)
                    last = "b" if kind == "b" else ("n" if near else "m")
                    p.op("pe", lambda e: e.matmul(self.bk(b), kk[sl][r0:r0 + nr, 128 * kt:128 * kt + 128],
                                                  qs[sl][r0:r0 + nr, 512 * qt:512 * qt + 512], start=True,
                                                  stop=(last == "m")),
                         reads=[kkb[sl], qsb[sl]], writes=[self.bank[b]], inc=(last == "m"))
                    if kind == "b":
                        p.op("pe", lambda e: e.matmul(self.bk(b), krs[sl][:, 128 * kt:128 * kt + 128],
                                                      qrs[sl][:, 512 * qt:512 * qt + 512], start=False, stop=True),
                             reads=[krsb[sl], qrsb[sl]], writes=[self.bank[b]])
                    elif near:
                        p.op("pe", lambda e: e.matmul(self.bk(b), self.anti, gb[sl][:, dl + 1, :], start=False,
                                                      stop=True),
                             reads=[gbb[sl], self.cb2], writes=[self.bank[b]])
                    return b, (None if (kind == "b" or near) else (0 if dl < 0 else 1))
                pend = [smm(0), smm(1)]
                for kt in range(16):
                    b, fr = pend.pop(0)
                    if kt + 2 < 16:
                        pend.append(smm(kt + 2))
                    pi = us["p"] % NP
                    us["p"] += 1
                    if fr is None:
                        p.op("act", lambda e: e.activation(out=pT[pi][:], in_=self.bk(b), func=AF.Exp),
                             reads=[self.bank[b]], writes=[pTb[pi]])
                    else:
                        p.op("act", lambda e: e.activation(out=pT[pi][:], in_=self.bk(b), func=AF.Exp,
                                                           bias=far[:, 8 * fr + h:8 * fr + h + 1]),
                             reads=[self.bank[b], farb], writes=[pTb[pi]])
                    p.op("pe", lambda e: e.matmul(self.bk(bO), vs[sl][:, kt, :], pT[pi][:], start=(kt == 0),
                                                  stop=(kt == 15)),
                         reads=[vsb[sl], pTb[pi]], writes=[self.bank[bO]])
                    if kt == 0:
                        p.op("dve", lambda e: e.tensor_copy(out=acc[qi][:], in_=pT[pi][:]),
                             reads=[pTb[pi]], writes=[accb[qi]])
                    elif kt < 15:
                        p.op("dve", lambda e: e.tensor_tensor(out=acc[qi][:], in0=acc[qi][:], in1=pT[pi][:], op=ALU.add),
                             reads=[pTb[pi], accb[qi]], writes=[accb[qi]])
                    else:
                        p.op("dve", lambda e: e.tensor_tensor(out=acb[qi][:], in0=acc[qi][:], in1=pT[pi][:], op=ALU.add),
                             reads=[pTb[pi], accb[qi]], writes=[acbb[qi]])
                p.op("pe", lambda e: e.matmul(self.bk(bS), self.ones, acb[qi][:], start=True, stop=True),
                     reads=[acbb[qi], self.cb2], writes=[self.bank[bS]])
                p.op("dve", lambda e: e.reciprocal(out=rs[qi][:], in_=self.bk(bS)), reads=[self.bank[bS]],
                     writes=[rsb[qi]])
                p.op("dve", lambda e: e.tensor_tensor(out=on[oni][:, 512 * qt:512 * qt + 512], in0=self.bk(bO),
                                                      in1=rs[qi][:], op=ALU.mult),
                     reads=[self.bank[bO], rsb[qi]], writes=[onb[oni]])

        def vsrc(T, h):
            return T[h].rearrange("p (t d) -> p t d", d=128)

        def load_a(h, sl):
            p.dma("sp", qs[sl][:], AQK[h], reads=[bAQK[h]], writes=[qsb[sl]], key=qsb[sl])
            p.dma("sp", ks[sl][:], AQK[8 + h], reads=[bAQK[8 + h]], writes=[ksb[sl]], key=ksb[sl])
            p.dma("sp", krs[sl][:], AQK[16 + h], reads=[bAQK[16 + h]], writes=[krsb[sl]], key=krsb[sl])
            p.dma("sp", vs[sl][:], vsrc(VA, h), reads=[bVA[h]], writes=[vsb[sl]], key=vsb[sl])
            for dl in range(-1, 5):
                src = bass.AP(tensor=TR.tensor, offset=h * NR + 640 - 128 * dl, ap=[[1, 128], [1, 512]])
                p.dma("pool", gb[sl][:, dl + 1, :], src, reads=[bTR], writes=[gbb[sl]], key=gbb[sl])

        def load_b(h, sl):
            p.dma("sp", qs[sl][:], QN[h], reads=[bQN[h]], writes=[qsb[sl]], key=qsb[sl])
            p.dma("sp", ks[sl][:], KN[h], reads=[bKN[h]], writes=[ksb[sl]], key=ksb[sl])
            p.dma("sp", vs[sl][:], vsrc(VB, h), reads=[bVB[h]], writes=[vsb[sl]], key=vsb[sl])
            p.dma("sp", qrs[sl][:], QR[h], reads=[bQR[h]], writes=[qrsb[sl]], key=qrsb[sl])
            p.dma("sp", krs[sl][:], KR[h], reads=[bKR[h]], writes=[krsb[sl]], key=krsb[sl])

        heads = [("a", h) for h in range(8)] + [("b", h) for h in range(8)]
        (load_a if heads[0][0] == "a" else load_b)(heads[0][1], 0)
        for i, (kind, h) in enumerate(heads):
            sl = i % NH
            if i + 1 < len(heads):
                (load_a if heads[i + 1][0] == "a" else load_b)(heads[i + 1][1], (i + 1) % NH)
            o = us["o"] % 2
            us["o"] += 1
            if kind == "a":
                unit("a", h, 0, sl, 0)
                unit("a", h, 1, sl, 1)
                p.op("dve", lambda e: e.scalar_tensor_tensor(out=dd[:], in0=on[1][:], scalar=cs[:, 2:3], in1=on[0][:],
                                                             op0=ALU.mult, op1=ALU.add),
                     reads=[onb[0], onb[1], csb], writes=[ddb])
                for j in range(4):
                    k = us["d"] % 2
                    us["d"] += 1
                    bB = us["s"] % 4
                    us["s"] += 1
                    p.op("act", lambda e: e.activation(out=dsq[k][:], in_=dd[:, 512 * j:512 * j + 512], func=AF.Square),
                         reads=[ddb], writes=[dsqb[k]])
                    p.op("pe", lambda e: e.matmul(self.bk(bB), self.ones, dsq[k][:], start=True, stop=True),
                         reads=[dsqb[k], self.cb2], writes=[self.bank[bB]])
                    p.op("act", lambda e: e.activation(out=drr[k][:], in_=self.bk(bB), func=AF.Sqrt, scale=1.0 / 128,
                                                       bias=self.eps_ap()), reads=[self.bank[bB]], writes=[drrb[k]])
                    p.op("dve", lambda e: e.reciprocal(out=drr[k][:], in_=drr[k][:]), reads=[drrb[k]], writes=[drrb[k]])
                    p.op("dve", lambda e: e.scalar_tensor_tensor(out=oo[o][:, 512 * j:512 * j + 512],
                                                                 in0=dd[:, 512 * j:512 * j + 512], scalar=cs[:, 1:2],
                                                                 in1=drr[k][:], op0=ALU.mult, op1=ALU.mult),
                         reads=[ddb, drrb[k], csb], writes=[oob[o]])
                p.dma("sp", OT[h], oo[o][:], reads=[oob[o]], writes=[bOT[h]], key=oob[o])
            else:
                unit("b", h, 0, sl, 0)
                p.op("act", lambda e: e.activation(out=oo[o][:], in_=on[0][:], func=AF.Copy),
                     reads=[onb[0]], writes=[oob[o]])
                p.dma("sp", OT[8 + h], oo[o][:], reads=[oob[o]], writes=[bOT[8 + h]], key=oob[o])
        p.pop()

        if STG < 4:
            p.pop()
            return
        p.push()
        oT = p.sb("aoT", [128, 16, S], BF16)
        ob = [Buf() for _ in range(16)]
        for c in range(16):
            p.dma("sp", oT[:, c, :], OT[c], reads=[bOT[c]], writes=[ob[c]], key=ob[c])
        self.outproj(self.A["w_out"], oT, ob, xsrc, xsb, xdst, xdb)
        p.pop()
        p.pop()


def xbufs():
    return [[Buf() for _ in range(4)] for _ in range(16)]


def flat(bl):
    return bl


def build(phases):
    k = K(phases)
    p = k.p
    k.eps_ap()
    cur, curb = k.x_in, xbufs()
    seq = [ph for ph in ("attn", "ffn0", "conv", "ffn1") if ph in phases]
    scratch = list(k.xs)
    for i, ph in enumerate(seq):
        last = (i == len(seq) - 1)
        dst = k.x_out if last else scratch.pop(0)
        dstb = xbufs()
        if ph == "ffn0":
            k.ffn(0, cur, curb, dst, dstb)
        elif ph == "ffn1":
            k.ffn(1, cur, curb, dst, dstb)
        elif ph == "conv":
            k.conv(cur, curb, dst, dstb)
        elif ph == "attn":
            k.attn(cur, curb, dst, dstb)
        cur, curb = dst, dstb
    p.finish([b for row in curb for b in row], "sp")
    p.es.close()
    return k


_CACHE = {}


def run(I, phases=("attn", "ffn0", "conv", "ffn1"), cores=8, xT_override=None):
    sh = prep_shared(I)
    x = np.asarray(I["x"], dtype=np.float32)
    pos = np.asarray(I["positions"]).astype(np.int32)
    key = tuple(phases)
    if key not in _CACHE:
        _CACHE[key] = build(phases)
    k = _CACHE[key]
    in_maps = []
    for c in range(cores):
        m = dict(sh)
        xt = x[c].T if xT_override is None else xT_override[c]
        m["xT"] = np.ascontiguousarray(xt).reshape(16, 128, S)
        m["pos"] = pos[c][None, :].copy()
        in_maps.append(m)
    res = run_bass_kernel_spmd(k.nc, in_maps, core_ids=list(range(cores)))
    outs = [np.ascontiguousarray(r["yT"].reshape(D, S).T) for r in res.results]
    return np.stack(outs, 0)


def kernel(**inputs):
    return run(inputs).astype(np.float32)
```
